# Optimizing a Trainium2 kernel written in Bass

```python
import jax, jax.numpy as jnp
from jax import lax
import numpy as np


D_MODEL = 1024
BATCH = 16
SEQ = 2048
DEPTH = 2

A_HEADS = 4
A_KEY_DIM = 128
A_VAL_DIM = 128
A_KEY_WIDTH = A_HEADS * A_KEY_DIM
A_WIDTH = A_HEADS * A_VAL_DIM
HGRN_CHUNK = 64
B_WIDTH = D_MODEL - A_WIDTH
B_BLOCKS = 8
B_BLOCK_DIM = B_WIDTH // B_BLOCKS
B_CONV = 4
RG_C = 8.0
C_HEADS = 16
C_HEAD_DIM = D_MODEL // C_HEADS
Q_BLOCK = 128
D_FF = 2816
FFN_CONV = 3
EPS = 1e-6

N_EVEN = (DEPTH + 1) // 2
N_ODD = DEPTH // 2
EVEN_SIZES = (A_KEY_WIDTH, A_KEY_WIDTH, A_WIDTH, A_WIDTH, B_WIDTH, B_WIDTH)
EVEN_IN = sum(EVEN_SIZES)
ODD_SIZES = (D_MODEL, D_MODEL, D_MODEL, C_HEADS)
ODD_IN = sum(ODD_SIZES)

kernel_name = 'hybrid_hgrn2_rglru_fox_convffn'


def rms_norm(x, gain):
    xf = x.astype(jnp.float32)
    y = xf * lax.rsqrt(jnp.mean(xf * xf, axis=-1, keepdims=True) + EPS)
    return (y * gain.astype(jnp.float32)).astype(x.dtype)


def split_cols(t, sizes):
    return jnp.split(t, list(np.cumsum(sizes)[:-1]), axis=-1)


def causal_dwconv(x, w, b):
    K, C = w.shape
    y = lax.conv_general_dilated(x, w[:, None, :].astype(x.dtype), window_strides=(1,),
                                 padding=[(K - 1, 0)], dimension_numbers=('NWC', 'WIO', 'NWC'),
                                 feature_group_count=C)
    return y + b.astype(x.dtype)


def hgrn2_mix(q, f_logit, v, g, lb, norm_gain):
    Bsz, T, _ = q.shape
    N = T // HGRN_CHUNK

    def heads(t, d):
        return t.reshape(Bsz, N, HGRN_CHUNK, A_HEADS, d).transpose(0, 3, 1, 2, 4)

    forget = lb + (1.0 - lb) * jax.nn.sigmoid(f_logit.astype(jnp.float32))
    qh = heads(jax.nn.silu(q.astype(jnp.float32)), A_KEY_DIM)
    kh = heads(1.0 - forget, A_KEY_DIM)
    bcum = jnp.cumsum(heads(jnp.log(forget), A_KEY_DIM), axis=3)
    vh = heads(v.astype(jnp.float32), A_VAL_DIM)
    b_last = bcum[:, :, :, -1:]
    q_dec = qh * jnp.exp(bcum)
    scores = jnp.einsum('bhnck,bhnsk->bhncs', q_dec, kh * jnp.exp(-bcum))
    causal = jnp.tril(jnp.ones((HGRN_CHUNK, HGRN_CHUNK), dtype=bool))
    scores = jnp.where(causal, scores, 0.0)
    o_intra = jnp.einsum('bhncs,bhnsv->bhncv', scores, vh)
    chunk_upd = jnp.einsum('bhnck,bhncv->bhnkv', kh * jnp.exp(b_last - bcum), vh)
    chunk_dec = jnp.exp(b_last[:, :, :, 0])

    def step(S, inp):
        dec, upd = inp
        return dec[..., None] * S + upd, S

    S0 = jnp.zeros((Bsz, A_HEADS, A_KEY_DIM, A_VAL_DIM), jnp.float32)
    _, S_prev = lax.scan(step, S0, (jnp.moveaxis(chunk_dec, 2, 0), jnp.moveaxis(chunk_upd, 2, 0)))
    S_prev = jnp.moveaxis(S_prev, 0, 2)
    o = o_intra + jnp.einsum('bhnck,bhnkv->bhncv', q_dec, S_prev)
    o = o.transpose(0, 2, 3, 1, 4).reshape(Bsz, T, A_HEADS, A_VAL_DIM)
    o = o * lax.rsqrt(jnp.mean(o * o, axis=-1, keepdims=True) + EPS)
    o = o * norm_gain.astype(jnp.float32).reshape(A_HEADS, A_VAL_DIM)
    return o.reshape(Bsz, T, A_WIDTH) * jax.nn.sigmoid(g.astype(jnp.float32))


def block_diag(x, w, b):
    xb = x.reshape(x.shape[0], x.shape[1], B_BLOCKS, B_BLOCK_DIM)
    return jnp.einsum('btni,nij->btnj', xb, w.astype(jnp.float32)).reshape(x.shape) + b.astype(jnp.float32)


def rglru_mix(x_br, y_br, conv_w, conv_b, wa, ba, wx, bx, lam):
    xf = causal_dwconv(x_br, conv_w, conv_b).astype(jnp.float32)
    r = jax.nn.sigmoid(block_diag(xf, wa, ba))
    i = jax.nn.sigmoid(block_diag(xf, wx, bx))
    log_a = -RG_C * r * jax.nn.softplus(-lam.astype(jnp.float32))
    a = jnp.exp(log_a)
    u = jnp.sqrt(-jnp.expm1(2.0 * log_a)) * (i * xf)

    def combine(left, right):
        a1, b1 = left
        a2, b2 = right
        return a1 * a2, a2 * b1 + b2

    _, h = lax.associative_scan(combine, (a, u), axis=1)
    return h * jax.nn.gelu(y_br.astype(jnp.float32))


def fox_attention(q, k, v, f_logit):
    Bsz, T, _ = q.shape

    def heads(t):
        return t.astype(jnp.float32).reshape(Bsz, T, C_HEADS, C_HEAD_DIM).transpose(0, 2, 1, 3)

    qh = heads(q) * (C_HEAD_DIM ** -0.5)
    kh, vh = heads(k), heads(v)
    c = jnp.cumsum(jax.nn.log_sigmoid(f_logit.astype(jnp.float32)), axis=1).transpose(0, 2, 1)
    outs = []
    for blk in range(T // Q_BLOCK):
        s0 = blk * Q_BLOCK
        end = s0 + Q_BLOCK
        logits = jnp.einsum('bhqd,bhkd->bhqk', qh[:, :, s0:end], kh[:, :, :end])
        logits = logits + c[:, :, s0:end, None] - c[:, :, None, :end]
        mask = (s0 + jnp.arange(Q_BLOCK))[:, None] >= jnp.arange(end)[None, :]
        p = jax.nn.softmax(jnp.where(mask, logits, -jnp.inf), axis=-1)
        outs.append(jnp.einsum('bhqk,bhkd->bhqd', p, vh[:, :, :end]))
    o = jnp.concatenate(outs, axis=2)
    return o.transpose(0, 2, 1, 3).reshape(Bsz, T, D_MODEL)


def conv_ffn(x, w_up, conv_w, conv_b, w_down):
    hid = causal_dwconv(x @ w_up, conv_w, conv_b)
    gate, val = jnp.split(hid, 2, axis=-1)
    return (jax.nn.gelu(gate) * val) @ w_down


def setup_inputs(seed: int = 0) -> dict:
    key = jax.random.key(seed)
    ks = jax.random.split(key, 20)
    f32 = jnp.float32
    nrm = jax.random.normal
    a0 = jax.random.uniform(ks[11], (N_EVEN, B_WIDTH), f32, 0.9 ** (1.0 / RG_C), 0.999 ** (1.0 / RG_C))
    return {
        'x': nrm(ks[0], (BATCH, SEQ, D_MODEL), f32),
        'norm_gains': 1.0 + 0.1 * nrm(ks[1], (DEPTH, 4, D_MODEL), f32),
        'even_w_in': nrm(ks[2], (N_EVEN, D_MODEL, EVEN_IN), f32) * D_MODEL ** -0.5,
        'hgrn_lb_logits': 0.1 * nrm(ks[3], (DEPTH + 1, A_KEY_WIDTH), f32),
        'hgrn_norm': 1.0 + 0.1 * nrm(ks[4], (N_EVEN, A_WIDTH), f32),
        'rg_conv_w': nrm(ks[5], (N_EVEN, B_CONV, B_WIDTH), f32) * B_CONV ** -0.5,
        'rg_conv_b': 0.01 * nrm(ks[6], (N_EVEN, B_WIDTH), f32),
        'rg_wa': nrm(ks[7], (N_EVEN, B_BLOCKS, B_BLOCK_DIM, B_BLOCK_DIM), f32) * B_BLOCK_DIM ** -0.5,
        'rg_ba': 0.01 * nrm(ks[8], (N_EVEN, B_WIDTH), f32),
        'rg_wx': nrm(ks[9], (N_EVEN, B_BLOCKS, B_BLOCK_DIM, B_BLOCK_DIM), f32) * B_BLOCK_DIM ** -0.5,
        'rg_bx': 0.01 * nrm(ks[10], (N_EVEN, B_WIDTH), f32),
        'rg_lambda': jnp.log(a0) - jnp.log1p(-a0),
        'even_w_out': nrm(ks[12], (N_EVEN, A_WIDTH + B_WIDTH, D_MODEL), f32) * (A_WIDTH + B_WIDTH) ** -0.5,
        'odd_w_in': nrm(ks[13], (N_ODD, D_MODEL, ODD_IN), f32) * D_MODEL ** -0.5,
        'fox_f_bias': jax.random.uniform(ks[14], (N_ODD, C_HEADS), f32, 1.0, 5.0),
        'odd_w_out': nrm(ks[15], (N_ODD, D_MODEL, D_MODEL), f32) * D_MODEL ** -0.5,
        'ffn_w_up': nrm(ks[16], (DEPTH, D_MODEL, 2 * D_FF), f32) * D_MODEL ** -0.5,
        'ffn_conv_w': nrm(ks[17], (DEPTH, FFN_CONV, 2 * D_FF), f32) * FFN_CONV ** -0.5,
        'ffn_conv_b': 0.01 * nrm(ks[18], (DEPTH, 2 * D_FF), f32),
        'ffn_w_down': nrm(ks[19], (DEPTH, D_FF, D_MODEL), f32) * D_FF ** -0.5,
    }


def reference(x, norm_gains, even_w_in, hgrn_lb_logits, hgrn_norm, rg_conv_w, rg_conv_b, rg_wa, rg_ba,
              rg_wx, rg_bx, rg_lambda, even_w_out, odd_w_in, fox_f_bias, odd_w_out,
              ffn_w_up, ffn_conv_w, ffn_conv_b, ffn_w_down):
    lb_all = jnp.cumsum(jax.nn.softmax(hgrn_lb_logits.astype(jnp.float32), axis=0), axis=0)
    for l in range(DEPTH):
        g = norm_gains[l]
        h = rms_norm(x, g[0])
        if l % 2 == 0:
            e = l // 2
            qa, fa, ia, ga, xb, yb = split_cols(h @ even_w_in[e], EVEN_SIZES)
            oa = hgrn2_mix(qa, fa, ia, ga, lb_all[l], hgrn_norm[e])
            ob = rglru_mix(xb, yb, rg_conv_w[e], rg_conv_b[e], rg_wa[e], rg_ba[e], rg_wx[e], rg_bx[e], rg_lambda[e])
            mix = jnp.concatenate([oa, ob], axis=-1).astype(h.dtype) @ even_w_out[e]
        else:
            o = l // 2
            qc, kc, vc, fc = split_cols(h @ odd_w_in[o], ODD_SIZES)
            mix = fox_attention(qc, kc, vc, fc + fox_f_bias[o]).astype(h.dtype) @ odd_w_out[o]
        x = x + rms_norm(mix, g[1])
        h = rms_norm(x, g[2])
        x = x + rms_norm(conv_ffn(h, ffn_w_up[l], ffn_conv_w[l], ffn_conv_b[l], ffn_w_down[l]), g[3])
    return x
```

```python
import numpy as np
import concourse.bass as bass
import concourse.mybir as mybir
from concourse.bass_utils import run_bass_kernel_spmd

F32 = mybir.dt.float32
BF16 = mybir.dt.bfloat16
AF = mybir.ActivationFunctionType
ALU = mybir.AluOpType
AX = mybir.AxisListType

ENGS = ("pe", "act", "dve", "pool", "sp")


class Prog:
    def __init__(self, nc, n_dma_sems=24):
        self.nc = nc
        self.streams = {e: [] for e in ENGS}
        self.eobj = {"pe": nc.tensor, "act": nc.scalar, "dve": nc.vector, "pool": nc.gpsimd, "sp": nc.sync}
        self.sem = {}
        self.tick = {}
        self._ctx = []
        for e in ENGS:
            g = nc.semaphore("s_" + e)
            self.sem[e] = g.__enter__()
            self._ctx.append(g)
            self.tick[e] = 0
        self.dma_sems = []
        for i in range(n_dma_sems):
            g = nc.semaphore("s_dma%d" % i)
            self.dma_sems.append(g.__enter__())
            self._ctx.append(g)
        self.dma_val = [0] * n_dma_sems
        self.dma_rr = 0
        self.dma_rr_q = {}
        self.semobj = {}
        for e in ENGS:
            self.semobj[("e", e)] = self.sem[e]
        for i, s in enumerate(self.dma_sems):
            self.semobj[("d", i)] = s
        self.seen = {e: {} for e in ENGS}
        self.lastw = {}
        self.reads = {}
        self.ninst = 0

    def _deps(self, reads, writes):
        deps = {}

        def add(sk, v):
            if deps.get(sk, 0) < v:
                deps[sk] = v
        for k in reads:
            w = self.lastw.get(k)
            if w:
                add(*w)
        for k in writes:
            w = self.lastw.get(k)
            if w:
                add(*w)
            for sk, v in self.reads.get(k, {}).items():
                add(sk, v)
        return deps

    def _record(self, reads, writes, sk, val):
        for k in writes:
            self.lastw[k] = (sk, val)
            self.reads[k] = {}
        for k in reads:
            d = self.reads.setdefault(k, {})
            if d.get(sk, 0) < val:
                d[sk] = val

    def _emit_waits(self, eng, deps):
        seen = self.seen[eng]
        for sk, v in deps.items():
            if seen.get(sk, 0) >= v:
                continue
            if sk == ("e", eng) and (eng == "pe" or v > self.tick[eng]):
                continue
            seen[sk] = v
            so = self.semobj[sk]
            self.streams[eng].append(("wait", so, v))
            self.eobj[eng].wait_ge(so, v)

    def op(self, eng, fn, reads=(), writes=(), inc=True):
        deps = self._deps(reads, writes)
        self._emit_waits(eng, deps)
        if inc:
            self.tick[eng] += 1
            val = self.tick[eng]
            self.streams[eng].append(("op", None, self.sem[eng], 1))
            fn(self.eobj[eng]).then_inc(self.sem[eng], 1)
        else:
            val = self.tick[eng] + 1
            self.streams[eng].append(("op", None, None, 0))
            fn(self.eobj[eng])
        self._record(reads, writes, ("e", eng), val)
        self.ninst += 1
        return val

    def dma(self, q, fn, reads=(), writes=(), n=1, slot=None):
        if slot is None:
            half = len(self.dma_sems) // 2
            base = 0 if q == "pool" else half
            rr = self.dma_rr_q.get(q, 0)
            slot = base + rr
            self.dma_rr_q[q] = (rr + 1) % half
        sk = ("d", slot)
        deps = self._deps(reads, writes)
        prev = self.dma_val[slot]
        if prev:
            if deps.get(sk, 0) < prev:
                deps[sk] = prev
        self._emit_waits(q, deps)
        so = self.dma_sems[slot]
        for i in range(n):
            self.streams[q].append(("dma", None, i, so))
            fn(self.eobj[q], i).then_inc(so, 16)
        self.dma_val[slot] = prev + 16 * n
        val = self.dma_val[slot]
        self._record(reads, writes, sk, val)
        self.ninst += n
        return sk, val

    def wait_all(self, eng):
        deps = {}
        for e in ENGS:
            if self.tick[e]:
                deps[("e", e)] = self.tick[e]
        for i, v in enumerate(self.dma_val):
            if v:
                deps[("d", i)] = v
        self._emit_waits(eng, deps)

    def build(self):
        pass

    def close(self):
        for g in reversed(self._ctx):
            g.__exit__(None, None, None)

from contextlib import ExitStack

NCORES = 8
SPC = 2
T = 2048
D = 1024
KC = 8
NTB = 4
TBW = 512
DFF = 2816
NJ = 22
EPS = 1e-6

C_G = 0
C_LBL = 64
C_HN = 76
C_RCW = 80
C_RCB = 96
C_RBA = 100
C_RBX = 104
C_RLAM = 108
C_FCW = 112
C_FCB = 376
C_END = 464


def build_nc(nph=4, nseq=SPC):
    nc = bass.Bass("TRN2", target_bir_lowering=False)
    es = ExitStack()

    def din(name, shape):
        return nc.dram_tensor(name, list(shape), F32, kind="ExternalInput").ap()

    x_d = din("x", [SPC, T, D])
    ng_d = din("norm_gains", [2, 4, 1024])
    ewi_d = din("even_w_in", [1, 1024, 3072])
    lbl_d = din("hgrn_lb_logits", [3, 512])
    hn_d = din("hgrn_norm", [1, 512])
    rcw_d = din("rg_conv_w", [1, 4, 512])
    rcb_d = din("rg_conv_b", [1, 512])
    rwa_d = din("rg_wa", [1, 8, 64, 64])
    rba_d = din("rg_ba", [1, 512])
    rwx_d = din("rg_wx", [1, 8, 64, 64])
    rbx_d = din("rg_bx", [1, 512])
    rlam_d = din("rg_lambda", [1, 512])
    ewo_d = din("even_w_out", [1, 1024, 1024])
    owi_d = din("odd_w_in", [1, 1024, 3088])
    ffb_d = din("fox_f_bias", [1, 16])
    owo_d = din("odd_w_out", [1, 1024, 1024])
    fwu_d = din("ffn_w_up", [2, 1024, 5632])
    fcw_d = din("ffn_conv_w", [2, 3, 5632])
    fcb_d = din("ffn_conv_b", [2, 5632])
    fwd_d = din("ffn_w_down", [2, 2816, 1024])
    out_d = nc.dram_tensor("out", [SPC, T, D], F32, kind="ExternalOutput").ap()

    P = Prog(nc)

    def sb(name, shape, dt):
        return es.enter_context(nc.sbuf_tensor(name, list(shape), dt))

    def psum(name, shape, dt):
        return es.enter_context(nc.psum_tensor(name, list(shape), dt))

    xT = sb("xT", [128, KC, T], F32)
    hT = sb("hT", [128, KC, T], BF16)
    G = sb("G", [128, 22528], BF16)
    WS = [sb("ws%d" % i, [128, 2816], BF16) for i in range(3)]
    YS = sb("YS", [128, 8, 512], F32)
    TF = sb("TF", [128, 4, 516], F32)
    TB_ = sb("TB", [128, 12, 512], BF16)
    PT = sb("PT", [128, 464], F32)
    ident = sb("ident", [128, 128], F32)
    identb = sb("identb", [128, 128], BF16)
    onesD = sb("onesD", [128, 128], BF16)
    onesH = sb("onesH", [128, 128], BF16)
    maskh = sb("maskh", [128, 512], BF16)
    tri = sb("tri", [128, 128], BF16)
    cmask = sb("cmask", [128, 512], BF16)
    onesF = sb("onesF", [128, 512], BF16)
    wabd = sb("wabd", [128, 4, 128], BF16)
    wxbd = sb("wxbd", [128, 4, 128], BF16)
    wfb = sb("wfb", [128, KC, 16], BF16)
    selc = sb("selc", [16, 17, 65], BF16)
    small = sb("small", [128, 64], F32)
    Sst = sb("Sst", [128, 128], F32)
    decs = sb("decs", [128, 2, 8], F32)
    Sb = sb("Sb", [128, 9, 128], BF16)
    hlast = sb("hlast", [128, 4], F32)
    fhalo = sb("fhalo", [128, 44, 2], F32)
    rhalo = sb("rhalo", [128, 4, 3], F32)
    negc = sb("negc", [128, 16, 16], F32)
    clast = sb("clast", [16, 1], F32)

    pb = [psum("pb%d" % i, [128, 512], F32) for i in range(7)]
    pbb = psum("pbb", [128, 1024], BF16)
    bank_rr = [0]

    held = set()

    def bank(hold=False):
        for _ in range(8):
            i = bank_rr[0]
            bank_rr[0] = (i + 1) % 7
            if i not in held:
                break
        else:
            raise RuntimeError("no free psum bank")
        if hold:
            held.add(i)
        return pb[i], ("pb", i)

    def release(key):
        held.discard(key[1])

    def mixT(c, tb):
        return G[:, c * 2048 + tb * 512: c * 2048 + (tb + 1) * 512], ("G", 4 * c + tb)

    def gvT(kc, sub):
        return G[:, kc * 1024 + sub * 512: kc * 1024 + (sub + 1) * 512], ("G", 2 * kc + sub)

    qaug = G[:, 16384:18432]
    kaug = G[:, 18432:20480]
    crefT = G[0:16, 20480:22528]
    QK = [("G", g) for g in range(32, 36)]
    KK = [("G", g) for g in range(36, 40)]
    YSb = YS[:].rearrange("p a b -> p (a b)").bitcast(BF16)
    vaug = YSb[:, 4096:8192].rearrange("p (t e c) -> p t e c", t=16, e=2)
    VK = [("ys", i) for i in range(4, 8)]
    cT = YS[:].rearrange("p a b -> p (a b)")[0:16, 0:2048]

    def tf(i):
        if i < 8:
            return YS[:, i, :], ("ys", i)
        return TF[:, i - 8, 0:512], ("tf", i - 8)

    def tb_(i):
        return TB_[:, i, :], ("tbf", i)

    def xk(c, tb):
        return ("x", c, tb)

    def hk(c, tb):
        return ("h", c, tb)

    def pcol(c):
        return PT[:, c:c + 1]

    act = lambda fn, r, w: P.op("act", fn, r, w)
    dve = lambda fn, r, w: P.op("dve", fn, r, w)
    pool = lambda fn, r, w: P.op("pool", fn, r, w)

    def mm(out, lhsT, rhs, start, stop, r, w, last=None):
        if last is None:
            last = stop
        P.op("pe", lambda e: e.matmul(out, lhsT=lhsT, rhs=rhs, start=start, stop=stop), r, w, inc=last)

    def tr(out, in_, idn, r, w, last=True):
        P.op("pe", lambda e: e.transpose(out=out, in_=in_, identity=idn), r, w, inc=last)

    ws_n = [0]
    plan_q = []
    issued = []

    def plan(specs):
        plan_q.extend(specs)

    def _issue():
        parts = plan_q.pop(0)
        i = ws_n[0] % 3
        ws_n[0] += 1
        buf = WS[i]
        key = ("ws", i)

        def fn(e, j):
            src, kcn, off, W, ncols = parts[j]
            dst = buf[:, 0:kcn * W].rearrange("p (k w) -> p k w", k=kcn)[:, :, off:off + ncols]
            return e.dma_start(out=dst, in_=src.rearrange("(k p) n -> p k n", p=128))
        P.dma("pool", fn, reads=(), writes=[key], n=len(parts))
        issued.append((buf, key))

    def prefetch():
        while len(issued) < 3 and plan_q:
            _issue()

    def next_slab():
        while len(issued) < 3 and plan_q:
            _issue()
        return issued.pop(0)

    def setup():
        pool(lambda e: e.memset(ident[:], 0.0), [], ["ident"])
        pool(lambda e: e.affine_select(out=ident[:], in_=ident[:], pattern=[[-1, 128]], compare_op=ALU.not_equal,
                                       fill=1.0, base=0, channel_multiplier=1), ["ident"], ["ident"])
        pool(lambda e: e.tensor_copy(out=identb[:], in_=ident[:]), ["ident"], ["identb"])
        pool(lambda e: e.memset(onesD[:], 1.0 / 1024.0), [], ["onesD"])
        pool(lambda e: e.memset(onesH[:], 1.0 / 128.0), [], ["onesH"])
        pool(lambda e: e.memset(onesF[:], 1.0), [], ["onesF"])
        pool(lambda e: e.memset(tri[:], 1.0), [], ["tri"])
        pool(lambda e: e.affine_select(out=tri[:], in_=tri[:], pattern=[[1, 128]], compare_op=ALU.is_ge,
                                       fill=0.0, base=0, channel_multiplier=-1), ["tri"], ["tri"])
        pool(lambda e: e.memset(maskh[:], 1.0), [], ["maskh"])
        for r in range(4):
            pool(lambda e, r=r: e.affine_select(out=maskh[:, r * 128:(r + 1) * 128], in_=maskh[:, r * 128:(r + 1) * 128],
                                                pattern=[[1, 128]], compare_op=ALU.is_ge, fill=0.0, base=0,
                                                channel_multiplier=-1), ["maskh"], ["maskh"])
            pool(lambda e, r=r: e.memset(maskh[0:64, r * 128 + 64:(r + 1) * 128], 0.0), ["maskh"], ["maskh"])
        pool(lambda e: e.memset(cmask[:], 1.0), [], ["cmask"])
        pool(lambda e: e.memset(cmask[:].rearrange("p (a b) -> p a b", b=64)[:, :, 0:1], 0.0), ["cmask"], ["cmask"])
        pool(lambda e: e.memset(selc[:], 0.0), [], ["selc"])
        pool(lambda e: e.affine_select(out=selc[:, 0:16, 64], in_=selc[:, 0:16, 64], pattern=[[-1, 16]],
                                       compare_op=ALU.not_equal, fill=8.0, base=0, channel_multiplier=1),
             ["selc"], ["selc"])
        pool(lambda e: e.memset(wabd[:], 0.0), [], ["wabd"])
        pool(lambda e: e.memset(wxbd[:], 0.0), [], ["wxbd"])

        def bd(e, j):
            which, rc, blk = j // 8, (j % 8) // 2, j % 2
            dst = (wabd if which == 0 else wxbd)[blk * 64:(blk + 1) * 64, rc, blk * 64:(blk + 1) * 64]
            src = (rwa_d if which == 0 else rwx_d)[0, 2 * rc + blk]
            return e.dma_start(out=dst, in_=src)
        P.dma("pool", bd, reads=(), writes=["wabd", "wxbd"], n=16)
        P.dma("pool", lambda e, j: e.dma_start(out=wfb[:], in_=owi_d[0, :, 3072:3088].rearrange("(k p) n -> p k n", p=128)),
              reads=(), writes=["wfb"])
        P.dma("sp", lambda e, j: e.dma_start(out=small[0:16, 40:41], in_=ffb_d[0].rearrange("(h a) -> h a", a=1)), reads=(), writes=["small_ffb"])
        stg = YS[:, 0, :].rearrange("p (g c) -> p g c", g=4)
        pool(lambda e: e.memset(YS[:, 0, :], 0.0), [], [("ys", 0)])
        rows = [
            (ng_d.rearrange("l j (c p) -> (l j c) p", p=128), C_G, 64),
            (lbl_d.rearrange("l (c p) -> (l c) p", p=128), C_LBL, 12),
            (hn_d.rearrange("l (c p) -> (l c) p", p=128), C_HN, 4),
            (rcw_d.rearrange("l k (c p) -> (l k c) p", p=128), C_RCW, 16),
            (rcb_d.rearrange("l (c p) -> (l c) p", p=128), C_RCB, 4),
            (rba_d.rearrange("l (c p) -> (l c) p", p=128), C_RBA, 4),
            (rbx_d.rearrange("l (c p) -> (l c) p", p=128), C_RBX, 4),
            (rlam_d.rearrange("l (c p) -> (l c) p", p=128), C_RLAM, 4),
            (fcw_d.rearrange("l k (c p) -> (l k c) p", p=128), C_FCW, 264),
            (fcb_d.rearrange("l (c p) -> (l c) p", p=128), C_FCB, 88),
        ]
        pieces = []
        for src, c0, n in rows:
            r = 0
            while r < n:
                col = c0 + r
                g, off = col // 128, col % 128
                m = min(n - r, 128 - off)
                pieces.append((src[r:r + m, :], g, off, m))
                r += m

        def pf(e, j):
            src, g, off, m = pieces[j]
            return e.dma_start(out=stg[off:off + m, g, :], in_=src)
        P.dma("sp", pf, reads=(), writes=[("ys", 0)], n=len(pieces))
        bk, bkk = bank()
        for g in range(4):
            tr(bk[:, g * 128:(g + 1) * 128], stg[:, g, :], ident[:], [("ys", 0), "ident"], [bkk], last=(g == 3))
        act(lambda e: e.activation(out=PT[:], in_=bk[:, 0:464], func=AF.Copy), [bkk], ["PT"])
        act(lambda e: e.activation(out=small[:, 8:20], in_=PT[:, C_LBL:C_LBL + 12], func=AF.Exp), ["PT"], ["small"])
        dve(lambda e: e.tensor_tensor(out=small[:, 20:24], in0=small[:, 8:12], in1=small[:, 12:16], op=ALU.add), ["small"], ["small"])
        dve(lambda e: e.tensor_tensor(out=small[:, 20:24], in0=small[:, 20:24], in1=small[:, 16:20], op=ALU.add), ["small"], ["small"])
        dve(lambda e: e.reciprocal(out=small[:, 20:24], in_=small[:, 20:24]), ["small"], ["small"])
        dve(lambda e: e.tensor_tensor(out=small[:, 0:4], in0=small[:, 8:12], in1=small[:, 20:24], op=ALU.mult), ["small"], ["small"])
        dve(lambda e: e.tensor_scalar(out=small[:, 4:8], in0=small[:, 0:4], scalar1=-1.0, scalar2=1.0, op0=ALU.mult, op1=ALU.add), ["small"], ["small"])
        act(lambda e: e.activation(out=small[:, 32:36], in_=PT[:, C_RLAM:C_RLAM + 4], func=AF.Exp, scale=-1.0), ["PT", "small"], ["small"])
        act(lambda e: e.activation(out=small[:, 32:36], in_=small[:, 32:36], func=AF.Ln, bias=1.0, scale=1.0), ["small"], ["small"])
        dve(lambda e: e.tensor_scalar(out=small[:, 24:28], in0=small[:, 32:36], scalar1=-8.0, scalar2=None, op0=ALU.mult), ["small"], ["small"])
        dve(lambda e: e.tensor_scalar(out=small[:, 28:32], in0=small[:, 32:36], scalar1=-16.0, scalar2=None, op0=ALU.mult), ["small"], ["small"])

    SM = ["small"]
    def setup2():
        dve(lambda e: e.tensor_scalar(out=small[:, 44:48], in0=PT[:, C_RBA:C_RBA + 4], scalar1=-1.0, scalar2=None, op0=ALU.mult), ["PT", "small"], ["small"])
        dve(lambda e: e.tensor_scalar(out=small[:, 48:52], in0=PT[:, C_RBX:C_RBX + 4], scalar1=-1.0, scalar2=None, op0=ALU.mult), ["PT", "small"], ["small"])


    def load_x(s):
        for tb in range(NTB):
            def ld(e, j, tb=tb):
                return e.dma_start(out=YS[:, 2 * j:2 * j + 2, :].rearrange("p a b -> p (a b)"),
                                   in_=x_d[s, tb * 512 + j * 128: tb * 512 + (j + 1) * 128, :])
            P.dma("sp", ld, reads=(), writes=[("ys", i) for i in range(8)], n=4)
            for c in range(KC):
                bk, bkk = bank()
                for j in range(4):
                    src = YS[:, 2 * j:2 * j + 2, :].rearrange("p a b -> p (a b)")[:, c * 128:(c + 1) * 128]
                    tr(bk[:, j * 128:(j + 1) * 128], src, ident[:], [("ys", 2 * j), ("ys", 2 * j + 1), "ident"], [bkk], last=(j == 3))
                if c % 2 == 0:
                    act(lambda e, bk=bk, c=c, tb=tb: e.activation(out=xT[:, c, tb * 512:(tb + 1) * 512], in_=bk[:], func=AF.Copy), [bkk], [xk(c, tb)])
                else:
                    dve(lambda e, bk=bk, c=c, tb=tb: e.tensor_copy(out=xT[:, c, tb * 512:(tb + 1) * 512], in_=bk[:]), [bkk], [xk(c, tb)])

    def store_x(s):
        for tt in range(16):
            tb = tt // 4
            st = YS[:, 2 * (tt % 2):2 * (tt % 2) + 2, :].rearrange("p a b -> p (a b)")
            stk = [("ys", 2 * (tt % 2)), ("ys", 2 * (tt % 2) + 1)]
            for hf in range(2):
                bk, bkk = bank()
                for cc in range(4):
                    c = hf * 4 + cc
                    tr(bk[:, cc * 128:(cc + 1) * 128], xT[:, c, tt * 128:(tt + 1) * 128], ident[:], [xk(c, tb), "ident"], [bkk], last=(cc == 3))
                if hf == 0:
                    act(lambda e, bk=bk, st=st: e.activation(out=st[:, 0:512], in_=bk[:], func=AF.Copy), [bkk], [stk[0]])
                else:
                    dve(lambda e, bk=bk, st=st: e.tensor_copy(out=st[:, 512:1024], in_=bk[:]), [bkk], [stk[1]])
            P.dma("sp", lambda e, j, st=st, tt=tt: e.dma_start(out=out_d[s, tt * 128:(tt + 1) * 128, :], in_=st), reads=stk, writes=())

    def rstd_from(bk, bkk, dst, dk):
        act(lambda e: e.activation(out=dst, in_=bk[:], func=AF.Ln, bias=EPS, scale=1.0), [bkk], [dk])
        act(lambda e: e.activation(out=dst, in_=dst, func=AF.Exp, scale=-0.5), [dk], [dk])

    def prenorm(gcol, tbs=(0, 1, 2, 3), sqids=(0, 1, 2, 3)):
        for tb in tbs:
            sl = slice(tb * 512, (tb + 1) * 512)
            bk, bkk = bank(hold=True)
            pend = []

            def flush(n_keep):
                while len(pend) > n_keep:
                    c_, sq_, sqk_ = pend.pop(0)
                    mm(bk[:], onesD[:], sq_, c_ == 0, c_ == KC - 1, [sqk_, "onesD"], [bkk], last=True)
            for c in range(KC):
                sq, sqk = tb_(sqids[c % 4])
                if c % 4 == 3:
                    pool(lambda e, sq=sq, c=c: e.tensor_tensor(out=sq, in0=xT[:, c, sl], in1=xT[:, c, sl], op=ALU.mult), [xk(c, tb)], [sqk])
                else:
                    act(lambda e, sq=sq, c=c: e.activation(out=sq, in_=xT[:, c, sl], func=AF.Square), [xk(c, tb)], [sqk])
                pend.append((c, sq, sqk))
                flush(2)
            flush(0)
            rs, rsk = tf(8)
            rstd_from(bk, bkk, rs, rsk)
            release(bkk)
            for c in range(KC):
                dve(lambda e, c=c: e.scalar_tensor_tensor(out=hT[:, c, sl], in0=xT[:, c, sl], scalar=pcol(gcol + c), in1=rs,
                                                         op0=ALU.mult, op1=ALU.mult), [xk(c, tb), rsk, "PT"], [hk(c, tb)])

    def resid(tb, gcol, banks):
        sl = slice(tb * 512, (tb + 1) * 512)
        return sl

    def ys_default(idx, dc):
        return YS[:, dc, :], [("ys", dc)]

    def proj_resid(tbs, gcol, nk, rhs_fn, slab_fn, ysf=ys_default):
        nt = len(tbs)
        sbs = [bank(hold=True) for _ in range(nt)]
        pend = []

        def flush(n_keep):
            while len(pend) > n_keep:
                i_, dc_, sq_, sqk_ = pend.pop(0)
                mm(sbs[i_][0][:], onesD[:], sq_, dc_ == 0, dc_ == KC - 1, [sqk_, "onesD"], [sbs[i_][1]], last=True)
        sqn = [0]
        for dc in range(KC):
            lf, wkey = slab_fn(dc)
            for i_ in range(nt):
                bk, bkk = bank()
                for k in range(nk):
                    ra, rk = rhs_fn(k, i_)
                    mm(bk[:], lf(k), ra, k == 0, k == nk - 1, [wkey, rk], [bkk])
                flush(nt)
                ya, yk = ysf(i_, dc)
                act(lambda e, bk=bk, ya=ya: e.activation(out=ya, in_=bk[:], func=AF.Copy), [bkk], yk)
                sq, sqk = tb_(sqn[0] % 4)
                sqn[0] += 1
                act(lambda e, bk=bk, sq=sq: e.activation(out=sq, in_=bk[:], func=AF.Square), [bkk], [sqk])
                pend.append((i_, dc, sq, sqk))
        flush(0)
        for i_, tb in enumerate(tbs):
            sl = slice(tb * 512, (tb + 1) * 512)
            sbk, sbkk = sbs[i_]
            rs, rsk = tf(8 + (i_ % 2))
            rstd_from(sbk, sbkk, rs, rsk)
            release(sbkk)
            for dc in range(KC):
                ya, yk = ysf(i_, dc)
                if dc % 2 == 0:
                    dve(lambda e, dc=dc, ya=ya, rs=rs: e.scalar_tensor_tensor(out=ya, in0=ya, scalar=pcol(gcol + dc), in1=rs,
                                                                            op0=ALU.mult, op1=ALU.mult), yk + [rsk, "PT"], yk)
                    pool(lambda e, dc=dc, ya=ya, sl=sl: e.tensor_tensor(out=xT[:, dc, sl], in0=xT[:, dc, sl], in1=ya, op=ALU.add),
                         yk + [xk(dc, tb)], [xk(dc, tb)])
                else:
                    dve(lambda e, dc=dc, ya=ya, rs=rs: e.tensor_tensor(out=ya, in0=ya, in1=rs, op=ALU.mult), yk + [rsk], yk)
                    dve(lambda e, dc=dc, ya=ya, sl=sl: e.scalar_tensor_tensor(out=xT[:, dc, sl], in0=ya, scalar=pcol(gcol + dc), in1=xT[:, dc, sl],
                                                                            op0=ALU.mult, op1=ALU.add), yk + [xk(dc, tb), "PT"], [xk(dc, tb)])

    def outproj(w_d, gcol):
        wv = w_d[0]
        plan([[(wv[:, s_ * 256:(s_ + 1) * 256], 8, 0, 256, 256)] for _p in range(2) for s_ in range(4)])
        for pp in range(2):
            slabs = {}
            tbs = [2 * pp, 2 * pp + 1]

            def slab_fn(dc):
                s_ = dc // 2
                if s_ not in slabs:
                    slabs[s_] = next_slab()
                buf, key = slabs[s_]
                v = buf[:, 0:2048].rearrange("p (k w) -> p k w", k=8)
                o = (dc % 2) * 128
                return (lambda k: v[:, k, o:o + 128]), key

            def rhs_fn(k, idx, tbs=tbs):
                return mixT(k, tbs[idx])

            def ysf(idx, dc):
                if idx == 0:
                    return YS[:, dc, :], [("ys", dc)]
                return hT[:, dc, 0:1024].bitcast(F32), [hk(dc, 0), hk(dc, 1)]
            proj_resid(tbs, gcol, KC, rhs_fn, slab_fn, ysf)

    Gf = G[:, 16384:22528].bitcast(F32)

    def gf(i):
        return Gf[:, i * 512:(i + 1) * 512], [("G", 32 + 2 * i), ("G", 33 + 2 * i)]

    def next_slab_np():
        if not issued:
            _issue()
        return issued.pop(0)

    def hgrn_setup(hd):
        bufA, wkeyA = next_slab_np()
        bufB, wkeyB = next_slab_np()
        ctx = dict(hd=hd, WA=bufA[:, 0:2048].rearrange("p (k w) -> p k w", k=8), kA=wkeyA,
                   WB=bufB[:, 0:2048].rearrange("p (k w) -> p k w", k=8), kB=wkeyB)
        pool(lambda e: e.memset(Sst[:], 0.0), [], ["Sst"])
        return ctx

    def fset(p):
        ids = (4, 5, 6, 7) if p == 0 else (1, 9, 10, 11)
        return [tb_(i) for i in ids]

    def hgrn_inproj(ctx, tb, coff):
        W, wkey = (ctx["WA"], ctx["kA"]) if coff < 256 else (ctx["WB"], ctx["kB"])
        co = coff % 256
        sl = slice(tb * 512, (tb + 1) * 512)
        bk, bkk = bank(hold=True)
        for k in range(KC):
            mm(bk[:], W[:, k, co:co + 128], hT[:, k, sl], k == 0, k == KC - 1, [wkey, hk(k, tb)], [bkk])
        return bk, bkk

    def hgrn_front(ctx, tb, p):
        hd = ctx["hd"]
        WB, wkeyB = ctx["WB"], ctx["kB"]
        lbc = small[:, hd:hd + 1]
        omlc = small[:, 4 + hd:5 + hd]
        (qd, kqd), (vt, kvt), (kt_, kkt), (sc, ksc) = fset(p)
        bk, bkk = hgrn_inproj(ctx, tb, 128)
        yield
        t_f, kf = tf(0)
        t_l, kl = tf(1)
        t_b, kb = tf(2)
        t_eb, keb = tf(3)
        t_x, kx = tf(4)
        act(lambda e: e.activation(out=t_f, in_=bk[:], func=AF.Exp, scale=-1.0), [bkk], [kf])
        release(bkk)
        bq, bqk = hgrn_inproj(ctx, tb, 0)
        yield
        act(lambda e: e.activation(out=t_f, in_=t_f, func=AF.Ln, bias=1.0, scale=1.0), [kf], [kf])
        yield
        act(lambda e: e.activation(out=t_f, in_=t_f, func=AF.Exp, scale=-1.0), [kf], [kf])
        yield
        dve(lambda e: e.tensor_scalar(out=t_f, in0=t_f, scalar1=omlc, scalar2=lbc, op0=ALU.mult, op1=ALU.add), [kf] + SM, [kf])
        yield
        act(lambda e: e.activation(out=t_l, in_=t_f, func=AF.Ln), [kf], [kl])
        t_q, kq = tf(5)
        act(lambda e: e.activation(out=t_q, in_=bq[:], func=AF.Exp, scale=-1.0), [bqk], [kq])
        yield
        dve(lambda e: e.tensor_scalar(out=t_f, in0=t_f, scalar1=-1.0, scalar2=1.0, op0=ALU.mult, op1=ALU.add), [kf, kl], [kf])
        dve(lambda e: e.tensor_tensor_scan(out=t_b, data0=cmask[:], data1=t_l, initial=0.0, op0=ALU.mult, op1=ALU.add), [kl, "cmask"], [kb])
        act(lambda e: e.activation(out=t_q, in_=t_q, func=AF.Ln, bias=1.0, scale=1.0), [kq], [kq])
        yield
        act(lambda e: e.activation(out=t_eb, in_=t_b, func=AF.Exp), [kb], [keb])
        act(lambda e: e.activation(out=t_l, in_=t_b, func=AF.Exp, scale=-1.0), [kb], [kl])
        yield
        for n in range(8):
            dve(lambda e, n=n: e.tensor_scalar(out=t_x[:, n * 64:(n + 1) * 64], in0=t_l[:, n * 64:(n + 1) * 64],
                                               scalar1=t_eb[:, n * 64 + 63:n * 64 + 64], scalar2=None, op0=ALU.mult), [kl, keb], [kx])
            if n % 4 == 3:
                yield
        act(lambda e: e.activation(out=t_q, in_=t_q, func=AF.Exp, scale=-1.0), [kq], [kq])
        pool(lambda e: e.tensor_copy(out=decs[:, p, :], in_=t_eb.rearrange("p (a b) -> p a b", b=64)[:, :, 63]), [keb], [("decs", p)])
        kd, kkd = tb_(2)
        ke, kke = tb_(3)
        dve(lambda e: e.tensor_tensor(out=kd, in0=t_f, in1=t_l, op=ALU.mult), [kf, kl], [kkd])
        dve(lambda e: e.tensor_tensor(out=ke, in0=t_f, in1=t_x, op=ALU.mult), [kf, kx], [kke])
        yield
        dve(lambda e: e.tensor_tensor(out=t_q, in0=bq[:], in1=t_q, op=ALU.mult), [bqk, kq], [kq])
        release(bqk)
        yield
        dve(lambda e: e.tensor_tensor(out=qd, in0=t_q, in1=t_eb, op=ALU.mult), [kq, keb], [kqd])
        yield
        bk, bkk = bank(hold=True)
        for j in range(4):
            for k in range(KC):
                mm(bk[:, j * 128:(j + 1) * 128], hT[:, k, tb * 512 + j * 128: tb * 512 + (j + 1) * 128], WB[:, k, 0:128],
                   k == 0, k == KC - 1, [wkeyB, hk(k, tb)], [bkk], last=(k == KC - 1 and j == 3))
        yield
        dve(lambda e: e.tensor_copy(out=vt, in_=bk[:]), [bkk], [kvt])
        release(bkk)
        yield
        for j in range(4):
            tr(pbb[:, j * 128:(j + 1) * 128], ke[:, j * 128:(j + 1) * 128], identb[:], [kke, "identb"], ["pbb"], last=(j == 3))
        dve(lambda e: e.tensor_copy(out=kt_, in_=pbb[:, 0:512]), ["pbb"], [kkt])
        yield
        bk, bkk = bank(hold=True)
        for j in range(4):
            mm(bk[:, j * 128:(j + 1) * 128], kd[:, j * 128:(j + 1) * 128], qd[:, j * 128:(j + 1) * 128], True, True,
               [kkd, kqd], [bkk], last=(j == 3))
        dve(lambda e: e.tensor_tensor(out=sc, in0=bk[:], in1=maskh[:], op=ALU.mult), [bkk, "maskh"], [ksc])
        release(bkk)
        yield

    def hgrn_back(ctx, tb, p):
        hd = ctx["hd"]
        (qd, kqd), (vt, kvt), (kt_, kkt), (sc, ksc) = fset(p)
        dk = ("decs", p)
        S2 = TF[:, 1, 0:128]
        S2k = ("tf", 1)
        stt = [(Sst[:], "Sst"), (S2, S2k)]
        bu0, bu0k = bank(hold=True)
        bu1, bu1k = bank(hold=True)
        for n in (0, 2, 4, 6, 1, 3, 5, 7):
            j, hf = n // 2, n % 2
            bu, buk = (bu0, bu0k) if hf == 0 else (bu1, bu1k)
            rows = slice(hf * 64, (hf + 1) * 64)
            mm(bu[:, j * 128:(j + 1) * 128], kt_[rows, j * 128:(j + 1) * 128], vt[rows, j * 128:(j + 1) * 128], True, True,
               [kkt, kvt], [buk], last=(j == 3))
        yield
        pool(lambda e: e.tensor_copy(out=Sb[:, 0, :], in_=Sst[:]), ["Sst"], ["Sb"])
        for n in range(8):
            bu, buk = (bu0, bu0k) if n % 2 == 0 else (bu1, bu1k)
            (si, sik), (so, sok) = stt[n % 2], stt[(n + 1) % 2]
            dve(lambda e, n=n, bu=bu, si=si, so=so: e.scalar_tensor_tensor(out=so, in0=si, scalar=decs[:, p, n:n + 1],
                                                                         in1=bu[:, (n // 2) * 128:(n // 2 + 1) * 128], op0=ALU.mult, op1=ALU.add),
                [sik, dk, buk], [sok])
            if n < 7:
                pool(lambda e, n=n, so=so: e.tensor_copy(out=Sb[:, n + 1, :], in_=so), [sok], ["Sb"])
            yield
        release(bu0k)
        release(bu1k)
        bg, bgk = hgrn_inproj(ctx, tb, 384)
        yield
        bo, bok = bank(hold=True)
        for j in range(4):
            mm(bo[:, j * 128:(j + 1) * 128], vt[:, j * 128:(j + 1) * 128], sc[:, j * 128:(j + 1) * 128], True, False, [kvt, ksc], [bok], last=False)
            mm(bo[:, j * 128:j * 128 + 64], Sb[:, 2 * j, :], qd[:, j * 128:j * 128 + 64], False, False, ["Sb", kqd], [bok], last=False)
            mm(bo[:, j * 128 + 64:(j + 1) * 128], Sb[:, 2 * j + 1, :], qd[:, j * 128 + 64:(j + 1) * 128], False, True, ["Sb", kqd], [bok], last=(j == 3))
        yield
        osq, kosq = tb_(8)
        act(lambda e: e.activation(out=osq, in_=bo[:], func=AF.Square), [bok], [kosq])
        yield
        t_g, kg = tf(7)
        act(lambda e: e.activation(out=t_g, in_=bg[:], func=AF.Exp, scale=-1.0), [bgk], [kg])
        release(bgk)
        bs, bsk = bank(hold=True)
        mm(bs[:], onesH[:], osq, True, True, [kosq, "onesH"], [bsk])
        yield
        act(lambda e: e.activation(out=t_g, in_=t_g, func=AF.Ln, bias=1.0, scale=1.0), [kg], [kg])
        t_r, kr = tf(6)
        act(lambda e: e.activation(out=t_r, in_=bs[:], func=AF.Ln, bias=EPS, scale=1.0), [bsk], [kr])
        release(bsk)
        yield
        act(lambda e: e.activation(out=t_r, in_=t_r, func=AF.Exp, scale=-0.5), [kr], [kr])
        act(lambda e: e.activation(out=t_g, in_=t_g, func=AF.Exp, scale=-1.0), [kg], [kg])
        yield
        t_o, ko = tf(9)
        dve(lambda e: e.scalar_tensor_tensor(out=t_o, in0=bo[:], scalar=pcol(C_HN + hd), in1=t_r, op0=ALU.mult, op1=ALU.mult),
            [bok, kr, "PT"], [ko])
        release(bok)
        yield
        mo, mok = mixT(hd, tb)
        dve(lambda e: e.tensor_tensor(out=mo, in0=t_o, in1=t_g, op=ALU.mult), [ko, kg], [mok])
        yield

    def rglru_setup(rc):
        buf, wkey = next_slab_np()
        pool(lambda e: e.memset(rhalo[:, rc, :], 0.0), [], ["rhalo"])
        pool(lambda e: e.memset(hlast[:, rc:rc + 1], 0.0), [], ["hlast"])
        return dict(rc=rc, W=buf[:, 0:2048].rearrange("p (k w) -> p k w", k=8), k=wkey)

    def rglru_tb(ctx, tb):
        rc, W, wkey = ctx["rc"], ctx["W"], ctx["k"]
        kxb = ("tf", 2)
        sl = slice(tb * 512, (tb + 1) * 512)
        bk, bkk = bank(hold=True)
        for k in range(KC):
            mm(bk[:], W[:, k, 0:128], hT[:, k, sl], k == 0, k == KC - 1, [wkey, hk(k, tb)], [bkk])
        yield
        pool(lambda e: e.tensor_copy(out=TF[:, 2, 0:3], in_=rhalo[:, rc, :]), ["rhalo"], [kxb])
        act(lambda e: e.activation(out=TF[:, 2, 3:515], in_=bk[:], func=AF.Copy), [bkk], [kxb])
        release(bkk)
        pool(lambda e: e.tensor_copy(out=rhalo[:, rc, :], in_=TF[:, 2, 512:515]), [kxb], ["rhalo"])
        yield
        xf, kxf = gf(0)
        act(lambda e: e.activation(out=xf, in_=TF[:, 2, 3:515], func=AF.Identity, scale=pcol(C_RCW + 3 * 4 + rc), bias=pcol(C_RCB + rc)),
            [kxb, "PT"], kxf)
        yield
        for kk in range(3):
            dve(lambda e, kk=kk: e.scalar_tensor_tensor(out=xf, in0=TF[:, 2, kk:kk + 512], scalar=pcol(C_RCW + kk * 4 + rc), in1=xf,
                                                       op0=ALU.mult, op1=ALU.add), [kxb, "PT"] + kxf, kxf)
            yield
        xfb, kxfb = tb_(0)
        pool(lambda e: e.tensor_copy(out=xfb, in_=xf), kxf, [kxfb])
        yield
        br, brk = bank(hold=True)
        mm(br[:], wabd[:, rc, :], xfb, True, True, [kxfb, "wabd"], [brk])
        bi, bik = bank(hold=True)
        mm(bi[:], wxbd[:, rc, :], xfb, True, True, [kxfb, "wxbd"], [bik])
        yield
        t_r, kr = gf(1)
        t_i, ki = gf(2)
        act(lambda e: e.activation(out=t_r, in_=br[:], func=AF.Exp, scale=-1.0, bias=small[:, 44 + rc:45 + rc]), [brk] + SM, kr)
        release(brk)
        act(lambda e: e.activation(out=t_i, in_=bi[:], func=AF.Exp, scale=-1.0, bias=small[:, 48 + rc:49 + rc]), [bik] + SM, ki)
        release(bik)
        yield
        by, byk = bank(hold=True)
        for k in range(KC):
            mm(by[:], W[:, k, 128:256], hT[:, k, sl], k == 0, k == KC - 1, [wkey, hk(k, tb)], [byk])
        act(lambda e: e.activation(out=t_r, in_=t_r, func=AF.Ln, bias=1.0, scale=1.0), kr, kr)
        act(lambda e: e.activation(out=t_i, in_=t_i, func=AF.Ln, bias=1.0, scale=1.0), ki, ki)
        yield
        act(lambda e: e.activation(out=t_r, in_=t_r, func=AF.Exp, scale=-1.0), kr, kr)
        act(lambda e: e.activation(out=t_i, in_=t_i, func=AF.Exp, scale=-1.0), ki, ki)
        t_g, kg = TF[:, 3, 0:512], ("tf", 3)
        act(lambda e: e.activation(out=t_g, in_=by[:], func=AF.Square), [byk], [kg])
        yield
        t_a, ka = gf(3)
        t_a2, ka2 = gf(4)
        act(lambda e: e.activation(out=t_a, in_=t_r, func=AF.Exp, scale=small[:, 24 + rc:25 + rc]), kr + SM, ka)
        act(lambda e: e.activation(out=t_a2, in_=t_r, func=AF.Exp, scale=small[:, 28 + rc:29 + rc]), kr + SM, ka2)
        dve(lambda e: e.tensor_tensor(out=t_i, in0=t_i, in1=xf, op=ALU.mult), ki + kxf, ki)
        dve(lambda e: e.tensor_scalar(out=t_g, in0=t_g, scalar1=0.044715, scalar2=1.0, op0=ALU.mult, op1=ALU.add), [kg], [kg])
        yield
        dve(lambda e: e.tensor_scalar(out=t_a2, in0=t_a2, scalar1=-1.0, scalar2=1.0, op0=ALU.mult, op1=ALU.add), ka2, ka2)
        dve(lambda e: e.tensor_scalar_max(out=t_a2, in0=t_a2, scalar1=1e-30), ka2, ka2)
        dve(lambda e: e.tensor_tensor(out=t_g, in0=by[:], in1=t_g, op=ALU.mult), [byk, kg], [kg])
        yield
        act(lambda e: e.activation(out=t_a2, in_=t_a2, func=AF.Ln), ka2, ka2)
        act(lambda e: e.activation(out=t_g, in_=t_g, func=AF.Exp, scale=-1.5957691216), [kg], [kg])
        yield
        act(lambda e: e.activation(out=t_a2, in_=t_a2, func=AF.Exp, scale=0.5), ka2, ka2)
        act(lambda e: e.activation(out=t_g, in_=t_g, func=AF.Ln, bias=1.0, scale=1.0), [kg], [kg])
        yield
        dve(lambda e: e.tensor_tensor(out=t_i, in0=t_i, in1=t_a2, op=ALU.mult), ki + ka2, ki)
        act(lambda e: e.activation(out=t_g, in_=t_g, func=AF.Exp, scale=-1.0), [kg], [kg])
        yield
        t_h, kh = gf(5)
        dve(lambda e: e.tensor_tensor_scan(out=t_h, data0=t_a, data1=t_i, initial=hlast[:, rc:rc + 1], op0=ALU.mult, op1=ALU.add),
            ka + ki + ["hlast"], kh)
        dve(lambda e: e.tensor_tensor(out=t_g, in0=by[:], in1=t_g, op=ALU.mult), [byk, kg], [kg])
        release(byk)
        yield
        pool(lambda e: e.tensor_copy(out=hlast[:, rc:rc + 1], in_=t_h[:, 511:512]), kh, ["hlast"])
        mo, mok = mixT(4 + rc, tb)
        dve(lambda e: e.tensor_tensor(out=mo, in0=t_h, in1=t_g, op=ALU.mult), kh + [kg], [mok])
        yield

    def interleave(gens):
        gens = list(gens)
        while gens:
            for g in list(gens):
                try:
                    next(g)
                except StopIteration:
                    gens.remove(g)

    def ffn(l):
        wu = fwu_d[l]
        wd = fwd_d[l]
        gcol = C_G + l * 32 + 3 * 8
        pool(lambda e: e.memset(fhalo[:], 0.0), [("fh", q_) for q_ in range(44)], [("fh", q_) for q_ in range(44)])
        ffn_rr = [0, 0]
        for hb in range(2):
            plan([[(wu[:, j * 128:(j + 1) * 128], 8, 0, 256, 128),
                   (wu[:, DFF + j * 128:DFF + (j + 1) * 128], 8, 128, 256, 128)] for j in range(NJ)])
            plan([[(wd[:, dc * 128:(dc + 1) * 128], NJ, 0, 128, 128)] for dc in range(KC)])
        prefetch()
        prenorm(C_G + l * 32 + 16, (0, 1))
        for hb in range(2):
            steps = [(j, sub) for j in range(NJ) for sub in range(2)]
            st = {}
            Wcur = [None, None]

            def S0(i):
                j, sub = steps[i]
                if sub == 0:
                    buf, wkey = next_slab()
                    Wcur[0] = buf[:, 0:2048].rearrange("p (k w) -> p k w", k=8)
                    Wcur[1] = wkey
                W, wkey = Wcur
                tb = hb * 2 + sub
                sl = slice(tb * 512, (tb + 1) * 512)
                rec = []
                for gv in range(2):
                    jj = gv * NJ + j
                    bk, bkk = bank()
                    for k in range(KC):
                        mm(bk[:], W[:, k, gv * 128:(gv + 1) * 128], hT[:, k, sl], k == 0, k == KC - 1, [wkey, hk(k, tb)], [bkk])
                    ts_ = ffn_rr[0] % 4
                    ffn_rr[0] += 1
                    tc_, ktc = tf(ffn_rr[1] % 8)
                    ffn_rr[1] += 1
                    rec.append((jj, bk, bkk, ts_, ("tf", ts_), ("fh", jj), tc_, ktc, C_FCW + l * 132 + jj))
                for (jj, bk, bkk, ts_, kty, fhk, tc_, ktc, cw) in rec:
                    pool(lambda e, ts_=ts_, jj=jj: e.tensor_copy(out=TF[:, ts_, 0:2], in_=fhalo[:, jj, :]), [fhk], [kty])
                for (jj, bk, bkk, ts_, kty, fhk, tc_, ktc, cw) in rec:
                    act(lambda e, bk=bk, ts_=ts_: e.activation(out=TF[:, ts_, 2:514], in_=bk[:], func=AF.Copy), [bkk], [kty])
                for (jj, bk, bkk, ts_, kty, fhk, tc_, ktc, cw) in rec:
                    pool(lambda e, ts_=ts_, jj=jj: e.tensor_copy(out=fhalo[:, jj, :], in_=TF[:, ts_, 512:514]), [kty], [fhk])
                for (jj, bk, bkk, ts_, kty, fhk, tc_, ktc, cw) in rec:
                    act(lambda e, ts_=ts_, tc_=tc_, cw=cw, jj=jj: e.activation(out=tc_, in_=TF[:, ts_, 2:514], func=AF.Identity,
                                                                            scale=pcol(cw + 88), bias=pcol(C_FCB + l * 44 + jj)),
                        [kty, "PT"], [ktc])
                st[i] = rec

            def S1(i):
                rec = st[i]
                for off_, cofs in ((1, 44), (0, 0)):
                    for (jj, bk, bkk, ts_, kty, fhk, tc_, ktc, cw) in rec:
                        dve(lambda e, ts_=ts_, tc_=tc_, cw=cw, off_=off_, cofs=cofs: e.scalar_tensor_tensor(
                            out=tc_, in0=TF[:, ts_, off_:off_ + 512], scalar=pcol(cw + cofs), in1=tc_, op0=ALU.mult, op1=ALU.add),
                            [kty, ktc, "PT"], [ktc])

            def S2(i):
                j, sub = steps[i]
                rec = st.pop(i)
                ga, gak = rec[0][6], rec[0][7]
                vb, vbk = rec[1][6], rec[1][7]
                act(lambda e: e.activation(out=ga, in_=ga, func=AF.Gelu_apprx_tanh), [gak], [gak])
                go, gok = gvT(j, sub)
                dve(lambda e: e.tensor_tensor(out=go, in0=ga, in1=vb, op=ALU.mult), [gak, vbk], [gok])
            n_ = len(steps)
            for t_ in range(n_ + 2):
                if hb == 0 and t_ == 12:
                    prenorm(C_G + l * 32 + 16, (2, 3))
                if t_ < n_:
                    S0(t_)
                if 0 <= t_ - 1 < n_:
                    S1(t_ - 1)
                if 0 <= t_ - 2 < n_:
                    S2(t_ - 2)
            def slab_fn(dc):
                buf, key = next_slab()
                v = buf[:, 0:NJ * 128].rearrange("p (k w) -> p k w", k=NJ)
                return (lambda k: v[:, k, :]), key

            def rhs_fn(k, idx):
                return gvT(k, idx)

            def ysf(idx, dc, hb=hb):
                if idx == 0:
                    return YS[:, dc, :], [("ys", dc)]
                return hT[:, dc, hb * 1024:(hb + 1) * 1024].bitcast(F32), [hk(dc, 2 * hb), hk(dc, 2 * hb + 1)]
            proj_resid([hb * 2, hb * 2 + 1], gcol, NJ, rhs_fn, slab_fn, ysf)

    def fox_plan():
        wv = owi_d[0]
        for pr_ in range(8):
            plan([[(wv[:, 2048 + pr_ * 128:2048 + (pr_ + 1) * 128], 8, 0, 128, 128)],
                  [(wv[:, pr_ * 128:(pr_ + 1) * 128], 8, 0, 256, 128),
                   (wv[:, 1024 + pr_ * 128:1024 + (pr_ + 1) * 128], 8, 128, 256, 128)]])

    def fox():
        wv = owi_d[0]
        pool(lambda e: e.memset(clast[:], 0.0), [], ["clast"])
        for tb in range(NTB):
            sl = slice(tb * 512, (tb + 1) * 512)
            bk, bkk = bank()
            for k in range(KC):
                mm(bk[0:16, :], wfb[:, k, :], hT[:, k, sl], k == 0, k == KC - 1, ["wfb", hk(k, tb)], [bkk])
            ls = TF[0:16, 0, 0:512]
            act(lambda e, bk=bk: e.activation(out=ls, in_=bk[0:16, :], func=AF.Sigmoid, bias=small[0:16, 40:41]), [bkk, "small_ffb"], [("tf", 0)])
            act(lambda e: e.activation(out=ls, in_=ls, func=AF.Ln), [("tf", 0)], [("tf", 0)])
            dve(lambda e, sl=sl: e.tensor_tensor_scan(out=cT[:, sl], data0=onesF[0:16, :], data1=ls, initial=clast[:, 0:1], op0=ALU.mult, op1=ALU.add),
                [("tf", 0), "onesF", "clast"], [("ys", tb)])
            pool(lambda e, tb=tb: e.tensor_copy(out=clast[:, 0:1], in_=cT[:, tb * 512 + 511:tb * 512 + 512]), [("ys", tb)], ["clast"])
            pool(lambda e, sl=sl: e.tensor_copy(out=crefT[:, sl], in_=cT[:, sl]), [("ys", tb)], [("G", 40 + tb)])
        bk, bkk = bank()
        for kt in range(16):
            tr(bk[:, kt * 16:(kt + 1) * 16], cT[:, kt * 128:(kt + 1) * 128], ident[0:16, 0:16], [("ys", kt // 4), "ident"], [bkk], last=(kt == 15))
        act(lambda e, bk=bk: e.activation(out=negc[:].rearrange("p a b -> p (a b)"), in_=bk[:, 0:256], func=AF.Copy, scale=-1.0), [bkk], ["negc"])
        pool(lambda e: e.memset(vaug[:, :, 0, 64:128], 1.0), [bkk], VK)
        pool(lambda e: e.memset(vaug[:, :, 1, 0:64], 1.0), [], VK)
        vaugB = YSb[:, 0:4096].rearrange("p (t e c) -> p t e c", t=16, e=2)
        VKB = [("ys", i) for i in range(0, 4)]
        pool(lambda e: e.memset(vaugB[:, :, 0, 64:128], 1.0), [], VKB)
        pool(lambda e: e.memset(vaugB[:, :, 1, 0:64], 1.0), [], VKB)
        VA, VKS = [vaug, vaugB], [VK, VKB]

        def vproj(pr_):
            va, vk = VA[pr_ % 2], VKS[pr_ % 2]
            bufV, wkeyV = next_slab()
            WV = bufV[:, 0:1024].rearrange("p (k w) -> p k w", k=8)
            for g4 in range(4):
                bk, bkk = bank(hold=True)
                for j in range(4):
                    tt = g4 * 4 + j
                    for k in range(KC):
                        mm(bk[:, j * 128:(j + 1) * 128], hT[:, k, tt * 128:(tt + 1) * 128], WV[:, k, 0:128], k == 0, k == KC - 1,
                           [wkeyV, hk(k, g4)], [bkk], last=(k == KC - 1 and j == 3))
                bv = bk[:].rearrange("p (j e c) -> p j e c", j=4, e=2)
                dve(lambda e, bv=bv, g4=g4, va=va: e.tensor_copy(out=va[:, g4 * 4:(g4 + 1) * 4, 0, 0:64], in_=bv[:, :, 0, :]), [bkk], vk)
                dve(lambda e, bv=bv, g4=g4, va=va: e.tensor_copy(out=va[:, g4 * 4:(g4 + 1) * 4, 1, 64:128], in_=bv[:, :, 1, :]), [bkk], vk)
                release(bkk)
        vproj(0)
        pool(lambda e: e.memset(kaug[64:65, :], 1.0), [], KK)
        qaug2 = TB_[:, 4:8, :].rearrange("p a b -> p (a b)")
        kaug2 = TB_[:, 8:12, :].rearrange("p a b -> p (a b)")
        QK2 = [("tbf", 4 + t) for t in range(4)]
        KK2 = [("tbf", 8 + t) for t in range(4)]
        pool(lambda e: e.memset(kaug2[64:65, :], 1.0), [], KK2)
        QA, KA, QKS, KKS = [qaug, qaug2], [kaug, kaug2], [QK, QK2], [KK, KK2]
        for pr in range(8):
            buf, wkey = next_slab()
            W = buf[:, 0:2048].rearrange("p (k w) -> p k w", k=8)
            for tb in range(NTB):
                sl = slice(tb * 512, (tb + 1) * 512)
                bq, bqk = bank()
                for k in range(KC):
                    mm(bq[:], W[:, k, 0:128], hT[:, k, sl], k == 0, k == KC - 1, [wkey, hk(k, tb)], [bqk])
                act(lambda e, bq=bq, sl=sl: e.activation(out=QA[0][0:64, sl], in_=bq[0:64, :], func=AF.Copy, scale=0.125), [bqk], [QKS[0][tb]])
                act(lambda e, bq=bq, sl=sl: e.activation(out=QA[1][0:64, sl], in_=bq[64:128, :], func=AF.Copy, scale=0.125), [bqk], [QKS[1][tb]])
                bkx, bkxk = bank()
                for k in range(KC):
                    mm(bkx[:], W[:, k, 128:256], hT[:, k, sl], k == 0, k == KC - 1, [wkey, hk(k, tb)], [bkxk])
                dve(lambda e, bkx=bkx, sl=sl: e.tensor_copy(out=KA[0][0:64, sl], in_=bkx[0:64, :]), [bkxk], [KKS[0][tb]])
                dve(lambda e, bkx=bkx, sl=sl: e.tensor_copy(out=KA[1][0:64, sl], in_=bkx[64:128, :]), [bkxk], [KKS[1][tb]])
            for e2 in range(2):
                h_ = pr * 2 + e2
                P.dma("sp", lambda e, j, e2=e2, h_=h_: e.dma_start(out=QA[e2][64:65, :], in_=crefT[h_:h_ + 1, :]),
                      reads=[("G", 40 + t) for t in range(4)], writes=QKS[e2])
            def kt_order(qb):
                n = 4 * qb + 4
                if qb == 0:
                    return list(range(4))
                return [0] + list(range(4 * qb, n)) + list(range(1, 4 * qb))
            its = [(e2, qb, kt) for e2 in range(2) for qb in range(4) for kt in kt_order(qb)]
            LA = 2
            sb_ = {}
            bos = {}
            for i in range(len(its) + LA):
                if i < len(its):
                    e2, qb, kt = its[i]
                    qaug_, kaug_, QK_, KK_ = QA[e2], KA[e2], QKS[e2], KKS[e2]
                    j = kt - 4 * qb
                    off = 128 * j if j > 0 else 0
                    N = 512 - off
                    bs, bsk = bank(hold=True)
                    mm(bs[:, 0:N], kaug_[0:65, kt * 128:(kt + 1) * 128], qaug_[0:65, qb * 512 + off:(qb + 1) * 512], True, True,
                       [KK_[kt // 4], QK_[qb]], [bsk])
                    sb_[i] = (bs, bsk, off, N, j)
                if i == 24 and pr + 1 < 8:
                    vproj(pr + 1)
                i2 = i - LA
                if i2 >= 0:
                    e2, qb, kt = its[i2]
                    h = pr * 2 + e2
                    order = kt_order(qb)
                    first, last_ = (kt == order[0]), (kt == order[-1])
                    bs, bsk, off, N, j = sb_.pop(i2)
                    if first:
                        bos[(e2, qb)] = bank(hold=True)
                    bo, bok = bos[(e2, qb)]
                    pt, ptk = tb_(i2 % 4)
                    act(lambda e, bs=bs, pt=pt, N=N, kt=kt, h=h: e.activation(out=pt[:, 0:N], in_=bs[:, 0:N], func=AF.Exp, bias=negc[:, kt, h:h + 1]),
                        [bsk, "negc"], [ptk])
                    release(bsk)
                    if j >= 0:
                        pool(lambda e, pt=pt: e.tensor_tensor(out=pt[:, 0:128], in0=pt[:, 0:128], in1=tri[:], op=ALU.mult), [ptk, "tri"], [ptk])
                    mm(bo[:, off:512], VA[pr % 2][:, kt, e2, :], pt[:, 0:N], first, last_, VKS[pr % 2] + [ptk], [bok], last=True)
                    if last_:
                        rr, rrk = tf(8 + (qb % 2))
                        cp, cpk = TF[:, 2 + (qb % 2), 0:512], ("tf", 2 + (qb % 2))
                        mo, mok = mixT(pr, qb)
                        dve(lambda e, bo=bo, cp=cp: e.tensor_copy(out=cp, in_=bo[:]), [bok], [cpk])
                        release(bok)
                        if e2 == 0:
                            act(lambda e, cp=cp, rr=rr: e.activation(out=rr[0:64, :], in_=cp[64:128, :], func=AF.Ln), [cpk], [rrk])
                            act(lambda e, rr=rr: e.activation(out=rr[0:64, :], in_=rr[0:64, :], func=AF.Exp, scale=-1.0), [rrk], [rrk])
                            dve(lambda e, cp=cp, mo=mo, rr=rr: e.tensor_tensor(out=mo[0:64, :], in0=cp[0:64, :], in1=rr[0:64, :], op=ALU.mult), [cpk, rrk], [mok])
                        else:
                            dve(lambda e, cp=cp, rr=rr: e.reciprocal(out=rr[64:128, :], in_=cp[0:64, :]), [cpk], [rrk])
                            dve(lambda e, cp=cp, mo=mo, rr=rr: e.tensor_tensor(out=mo[64:128, :], in0=cp[64:128, :], in1=rr[64:128, :], op=ALU.mult), [cpk, rrk], [mok])

    setup()
    setup2()
    for s in range(nseq):
        load_x(s)
        import os as _os
        parts = _os.environ.get("KPARTS", "hgrn,rglru,outproj").split(",")
        if nph >= 1:
            prenorm(C_G + 0, (0,))
            wv0 = ewi_d[0]
            hspecA = lambda hd: [(wv0[:, c0:c0 + 128], 8, j * 128, 256, 128) for j, c0 in enumerate([hd * 128, 512 + hd * 128])]
            hspecB = lambda hd: [(wv0[:, c0:c0 + 128], 8, j * 128, 256, 128) for j, c0 in enumerate([1024 + hd * 128, 1536 + hd * 128])]
            rspec = lambda rc: [(wv0[:, 2048 + rc * 128:2048 + (rc + 1) * 128], 8, 0, 256, 128),
                                (wv0[:, 2560 + rc * 128:2560 + (rc + 1) * 128], 8, 128, 256, 128)]
            for i_ in range(4):
                plan([hspecA(i_), hspecB(i_), rspec(i_)])
                ch = hgrn_setup(i_)
                cr = rglru_setup(i_)
                for st_ in range(5):
                    gens = []
                    if st_ >= 1:
                        gens.append(hgrn_back(ch, st_ - 1, (st_ - 1) % 2))
                    if st_ < 4:
                        gens.append(hgrn_front(ch, st_, st_ % 2))
                        gens.append(rglru_tb(cr, st_))
                    if i_ == 0 and st_ < 3:
                        prenorm(C_G + 0, (st_ + 1,), (0, 2, 3, 8))
                    interleave(gens)
            if "outproj" in parts:
                outproj(ewo_d, C_G + 8)
        if nph >= 2:
            ffn(0)
        if nph >= 3:
            fox_plan()
            prefetch()
            prenorm(C_G + 32)
            fox()
            outproj(owo_d, C_G + 32 + 8)
        if nph >= 4:
            ffn(1)
        store_x(s)
    P.wait_all("sp")
    P.build()
    return nc, P


_CACHE = {}


def kernel(**inputs):
    if "nc" not in _CACHE:
        _CACHE["nc"] = build_nc()[0]
    nc = _CACHE["nc"]
    x = np.ascontiguousarray(inputs["x"], dtype=np.float32)
    in_maps = []
    for c in range(NCORES):
        m = {k: np.ascontiguousarray(v, dtype=np.float32) for k, v in inputs.items() if k != "x"}
        m["x"] = np.ascontiguousarray(x[c * SPC:(c + 1) * SPC])
        in_maps.append(m)
    res = run_bass_kernel_spmd(nc, in_maps, core_ids=list(range(NCORES)))
    out = np.concatenate([np.asarray(r["out"]) for r in res.results], axis=0)
    return out.astype(np.float32)
```

```python
import numpy as np
import concourse.bass as bass
import concourse.mybir as mybir
from concourse.bass_utils import run_bass_kernel_spmd

F32 = mybir.dt.float32
BF16 = mybir.dt.bfloat16
AF = mybir.ActivationFunctionType
ALU = mybir.AluOpType
AX = mybir.AxisListType

ENGS = ("pe", "act", "dve", "pool", "sp")


class Prog:
    def __init__(self, nc, n_dma_sems=24):
        self.nc = nc
        self.streams = {e: [] for e in ENGS}
        self.eobj = {"pe": nc.tensor, "act": nc.scalar, "dve": nc.vector, "pool": nc.gpsimd, "sp": nc.sync}
        self.sem = {}
        self.tick = {}
        self._ctx = []
        for e in ENGS:
            g = nc.semaphore("s_" + e)
            self.sem[e] = g.__enter__()
            self._ctx.append(g)
            self.tick[e] = 0
        self.dma_sems = []
        for i in range(n_dma_sems):
            g = nc.semaphore("s_dma%d" % i)
            self.dma_sems.append(g.__enter__())
            self._ctx.append(g)
        self.dma_val = [0] * n_dma_sems
        self.dma_rr = 0
        self.dma_rr_q = {}
        self.semobj = {}
        for e in ENGS:
            self.semobj[("e", e)] = self.sem[e]
        for i, s in enumerate(self.dma_sems):
            self.semobj[("d", i)] = s
        self.seen = {e: {} for e in ENGS}
        self.lastw = {}
        self.reads = {}
        self.ninst = 0

    def _deps(self, reads, writes):
        deps = {}

        def add(sk, v):
            if deps.get(sk, 0) < v:
                deps[sk] = v
        for k in reads:
            w = self.lastw.get(k)
            if w:
                add(*w)
        for k in writes:
            w = self.lastw.get(k)
            if w:
                add(*w)
            for sk, v in self.reads.get(k, {}).items():
                add(sk, v)
        return deps

    def _record(self, reads, writes, sk, val):
        for k in writes:
            self.lastw[k] = (sk, val)
            self.reads[k] = {}
        for k in reads:
            d = self.reads.setdefault(k, {})
            if d.get(sk, 0) < val:
                d[sk] = val

    def _emit_waits(self, eng, deps):
        seen = self.seen[eng]
        for sk, v in deps.items():
            if seen.get(sk, 0) >= v:
                continue
            if sk == ("e", eng) and (eng == "pe" or v > self.tick[eng]):
                continue
            seen[sk] = v
            so = self.semobj[sk]
            self.streams[eng].append(("wait", so, v))
            self.eobj[eng].wait_ge(so, v)

    def op(self, eng, fn, reads=(), writes=(), inc=True):
        deps = self._deps(reads, writes)
        self._emit_waits(eng, deps)
        if inc:
            self.tick[eng] += 1
            val = self.tick[eng]
            self.streams[eng].append(("op", None, self.sem[eng], 1))
            fn(self.eobj[eng]).then_inc(self.sem[eng], 1)
        else:
            val = self.tick[eng] + 1
            self.streams[eng].append(("op", None, None, 0))
            fn(self.eobj[eng])
        self._record(reads, writes, ("e", eng), val)
        self.ninst += 1
        return val

    def dma(self, q, fn, reads=(), writes=(), n=1, slot=None):
        if slot is None:
            half = len(self.dma_sems) // 2
            base = 0 if q == "pool" else half
            rr = self.dma_rr_q.get(q, 0)
            slot = base + rr
            self.dma_rr_q[q] = (rr + 1) % half
        sk = ("d", slot)
        deps = self._deps(reads, writes)
        prev = self.dma_val[slot]
        if prev:
            if deps.get(sk, 0) < prev:
                deps[sk] = prev
        self._emit_waits(q, deps)
        so = self.dma_sems[slot]
        for i in range(n):
            self.streams[q].append(("dma", None, i, so))
            fn(self.eobj[q], i).then_inc(so, 16)
        self.dma_val[slot] = prev + 16 * n
        val = self.dma_val[slot]
        self._record(reads, writes, sk, val)
        self.ninst += n
        return sk, val

    def wait_all(self, eng):
        deps = {}
        for e in ENGS:
            if self.tick[e]:
                deps[("e", e)] = self.tick[e]
        for i, v in enumerate(self.dma_val):
            if v:
                deps[("d", i)] = v
        self._emit_waits(eng, deps)

    def build(self):
        pass

    def close(self):
        for g in reversed(self._ctx):
            g.__exit__(None, None, None)

from contextlib import ExitStack

NCORES = 8
SPC = 2
T = 2048
D = 1024
KC = 8
NTB = 4
TBW = 512
DFF = 2816
NJ = 22
EPS = 1e-6

C_G = 0
C_LBL = 64
C_HN = 76
C_RCW = 80
C_RCB = 96
C_RBA = 100
C_RBX = 104
C_RLAM = 108
C_FCW = 112
C_FCB = 376
C_END = 464


def build_nc(nph=4, nseq=SPC):
    nc = bass.Bass("TRN2", target_bir_lowering=False)
    es = ExitStack()

    def din(name, shape):
        return nc.dram_tensor(name, list(shape), F32, kind="ExternalInput").ap()

    x_d = din("x", [SPC, T, D])
    ng_d = din("norm_gains", [2, 4, 1024])
    ewi_d = din("even_w_in", [1, 1024, 3072])
    lbl_d = din("hgrn_lb_logits", [3, 512])
    hn_d = din("hgrn_norm", [1, 512])
    rcw_d = din("rg_conv_w", [1, 4, 512])
    rcb_d = din("rg_conv_b", [1, 512])
    rwa_d = din("rg_wa", [1, 8, 64, 64])
    rba_d = din("rg_ba", [1, 512])
    rwx_d = din("rg_wx", [1, 8, 64, 64])
    rbx_d = din("rg_bx", [1, 512])
    rlam_d = din("rg_lambda", [1, 512])
    ewo_d = din("even_w_out", [1, 1024, 1024])
    owi_d = din("odd_w_in", [1, 1024, 3088])
    ffb_d = din("fox_f_bias", [1, 16])
    owo_d = din("odd_w_out", [1, 1024, 1024])
    fwu_d = din("ffn_w_up", [2, 1024, 5632])
    fcw_d = din("ffn_conv_w", [2, 3, 5632])
    fcb_d = din("ffn_conv_b", [2, 5632])
    fwd_d = din("ffn_w_down", [2, 2816, 1024])
    out_d = nc.dram_tensor("out", [SPC, T, D], F32, kind="ExternalOutput").ap()

    P = Prog(nc)

    def sb(name, shape, dt):
        return es.enter_context(nc.sbuf_tensor(name, list(shape), dt))

    def psum(name, shape, dt):
        return es.enter_context(nc.psum_tensor(name, list(shape), dt))

    xT = sb("xT", [128, KC, T], F32)
    hT = sb("hT", [128, KC, T], BF16)
    G = sb("G", [128, 22528], BF16)
    WS = [sb("ws%d" % i, [128, 2816], BF16) for i in range(3)]
    YS = sb("YS", [128, 8, 512], F32)
    TF = sb("TF", [128, 4, 516], F32)
    TB_ = sb("TB", [128, 12, 512], BF16)
    PT = sb("PT", [128, 464], F32)
    ident = sb("ident", [128, 128], F32)
    identb = sb("identb", [128, 128], BF16)
    onesD = sb("onesD", [128, 128], BF16)
    onesH = sb("onesH", [128, 128], BF16)
    maskh = sb("maskh", [128, 512], BF16)
    tri = sb("tri", [128, 128], BF16)
    cmask = sb("cmask", [128, 512], BF16)
    onesF = sb("onesF", [128, 512], BF16)
    wabd = sb("wabd", [128, 4, 128], BF16)
    wxbd = sb("wxbd", [128, 4, 128], BF16)
    wfb = sb("wfb", [128, KC, 16], BF16)
    selc = sb("selc", [16, 17, 65], BF16)
    small = sb("small", [128, 64], F32)
    Sst = sb("Sst", [128, 128], F32)
    decs = sb("decs", [128, 2, 8], F32)
    Sb = sb("Sb", [128, 9, 128], BF16)
    hlast = sb("hlast", [128, 4], F32)
    fhalo = sb("fhalo", [128, 44, 2], F32)
    rhalo = sb("rhalo", [128, 4, 3], F32)
    negc = sb("negc", [128, 16, 16], F32)
    clast = sb("clast", [16, 1], F32)

    pb = [psum("pb%d" % i, [128, 512], F32) for i in range(7)]
    pbb = psum("pbb", [128, 1024], BF16)
    bank_rr = [0]

    held = set()

    def bank(hold=False):
        for _ in range(8):
            i = bank_rr[0]
            bank_rr[0] = (i + 1) % 7
            if i not in held:
                break
        else:
            raise RuntimeError("no free psum bank")
        if hold:
            held.add(i)
        return pb[i], ("pb", i)

    def release(key):
        held.discard(key[1])

    def mixT(c, tb):
        return G[:, c * 2048 + tb * 512: c * 2048 + (tb + 1) * 512], ("G", 4 * c + tb)

    def gvT(kc, sub):
        return G[:, kc * 1024 + sub * 512: kc * 1024 + (sub + 1) * 512], ("G", 2 * kc + sub)

    qaug = G[:, 16384:18432]
    kaug = G[:, 18432:20480]
    crefT = G[0:16, 20480:22528]
    QK = [("G", g) for g in range(32, 36)]
    KK = [("G", g) for g in range(36, 40)]
    YSb = YS[:].rearrange("p a b -> p (a b)").bitcast(BF16)
    vaug = YSb[:, 4096:8192].rearrange("p (t e c) -> p t e c", t=16, e=2)
    VK = [("ys", i) for i in range(4, 8)]
    cT = YS[:].rearrange("p a b -> p (a b)")[0:16, 0:2048]

    def tf(i):
        if i < 8:
            return YS[:, i, :], ("ys", i)
        return TF[:, i - 8, 0:512], ("tf", i - 8)

    def tb_(i):
        return TB_[:, i, :], ("tbf", i)

    def xk(c, tb):
        return ("x", c, tb)

    def hk(c, tb):
        return ("h", c, tb)

    def pcol(c):
        return PT[:, c:c + 1]

    act = lambda fn, r, w: P.op("act", fn, r, w)
    dve = lambda fn, r, w: P.op("dve", fn, r, w)
    pool = lambda fn, r, w: P.op("pool", fn, r, w)

    def mm(out, lhsT, rhs, start, stop, r, w, last=None):
        if last is None:
            last = stop
        P.op("pe", lambda e: e.matmul(out, lhsT=lhsT, rhs=rhs, start=start, stop=stop), r, w, inc=last)

    def tr(out, in_, idn, r, w, last=True):
        P.op("pe", lambda e: e.transpose(out=out, in_=in_, identity=idn), r, w, inc=last)

    ws_n = [0]
    plan_q = []
    issued = []

    def plan(specs):
        plan_q.extend(specs)

    def _issue():
        parts = plan_q.pop(0)
        i = ws_n[0] % 3
        ws_n[0] += 1
        buf = WS[i]
        key = ("ws", i)

        def fn(e, j):
            src, kcn, off, W, ncols = parts[j]
            dst = buf[:, 0:kcn * W].rearrange("p (k w) -> p k w", k=kcn)[:, :, off:off + ncols]
            return e.dma_start(out=dst, in_=src.rearrange("(k p) n -> p k n", p=128))
        P.dma("pool", fn, reads=(), writes=[key], n=len(parts))
        issued.append((buf, key))

    def prefetch():
        while len(issued) < 3 and plan_q:
            _issue()

    def next_slab():
        while len(issued) < 3 and plan_q:
            _issue()
        return issued.pop(0)

    def setup():
        pool(lambda e: e.memset(ident[:], 0.0), [], ["ident"])
        pool(lambda e: e.affine_select(out=ident[:], in_=ident[:], pattern=[[-1, 128]], compare_op=ALU.not_equal,
                                       fill=1.0, base=0, channel_multiplier=1), ["ident"], ["ident"])
        pool(lambda e: e.tensor_copy(out=identb[:], in_=ident[:]), ["ident"], ["identb"])
        pool(lambda e: e.memset(onesD[:], 1.0 / 1024.0), [], ["onesD"])
        pool(lambda e: e.memset(onesH[:], 1.0 / 128.0), [], ["onesH"])
        pool(lambda e: e.memset(onesF[:], 1.0), [], ["onesF"])
        pool(lambda e: e.memset(tri[:], 1.0), [], ["tri"])
        pool(lambda e: e.affine_select(out=tri[:], in_=tri[:], pattern=[[1, 128]], compare_op=ALU.is_ge,
                                       fill=0.0, base=0, channel_multiplier=-1), ["tri"], ["tri"])
        pool(lambda e: e.memset(maskh[:], 1.0), [], ["maskh"])
        for r in range(4):
            pool(lambda e, r=r: e.affine_select(out=maskh[:, r * 128:(r + 1) * 128], in_=maskh[:, r * 128:(r + 1) * 128],
                                                pattern=[[1, 128]], compare_op=ALU.is_ge, fill=0.0, base=0,
                                                channel_multiplier=-1), ["maskh"], ["maskh"])
            pool(lambda e, r=r: e.memset(maskh[0:64, r * 128 + 64:(r + 1) * 128], 0.0), ["maskh"], ["maskh"])
        pool(lambda e: e.memset(cmask[:], 1.0), [], ["cmask"])
        pool(lambda e: e.memset(cmask[:].rearrange("p (a b) -> p a b", b=64)[:, :, 0:1], 0.0), ["cmask"], ["cmask"])
        pool(lambda e: e.memset(selc[:], 0.0), [], ["selc"])
        pool(lambda e: e.affine_select(out=selc[:, 0:16, 64], in_=selc[:, 0:16, 64], pattern=[[-1, 16]],
                                       compare_op=ALU.not_equal, fill=8.0, base=0, channel_multiplier=1),
             ["selc"], ["selc"])
        pool(lambda e: e.memset(wabd[:], 0.0), [], ["wabd"])
        pool(lambda e: e.memset(wxbd[:], 0.0), [], ["wxbd"])

        def bd(e, j):
            which, rc, blk = j // 8, (j % 8) // 2, j % 2
            dst = (wabd if which == 0 else wxbd)[blk * 64:(blk + 1) * 64, rc, blk * 64:(blk + 1) * 64]
            src = (rwa_d if which == 0 else rwx_d)[0, 2 * rc + blk]
            return e.dma_start(out=dst, in_=src)
        P.dma("pool", bd, reads=(), writes=["wabd", "wxbd"], n=16)
        P.dma("pool", lambda e, j: e.dma_start(out=wfb[:], in_=owi_d[0, :, 3072:3088].rearrange("(k p) n -> p k n", p=128)),
              reads=(), writes=["wfb"])
        P.dma("sp", lambda e, j: e.dma_start(out=small[0:16, 40:41], in_=ffb_d[0].rearrange("(h a) -> h a", a=1)), reads=(), writes=["small_ffb"])
        stg = YS[:, 0, :].rearrange("p (g c) -> p g c", g=4)
        pool(lambda e: e.memset(YS[:, 0, :], 0.0), [], [("ys", 0)])
        rows = [
            (ng_d.rearrange("l j (c p) -> (l j c) p", p=128), C_G, 64),
            (lbl_d.rearrange("l (c p) -> (l c) p", p=128), C_LBL, 12),
            (hn_d.rearrange("l (c p) -> (l c) p", p=128), C_HN, 4),
            (rcw_d.rearrange("l k (c p) -> (l k c) p", p=128), C_RCW, 16),
            (rcb_d.rearrange("l (c p) -> (l c) p", p=128), C_RCB, 4),
            (rba_d.rearrange("l (c p) -> (l c) p", p=128), C_RBA, 4),
            (rbx_d.rearrange("l (c p) -> (l c) p", p=128), C_RBX, 4),
            (rlam_d.rearrange("l (c p) -> (l c) p", p=128), C_RLAM, 4),
            (fcw_d.rearrange("l k (c p) -> (l k c) p", p=128), C_FCW, 264),
            (fcb_d.rearrange("l (c p) -> (l c) p", p=128), C_FCB, 88),
        ]
        pieces = []
        for src, c0, n in rows:
            r = 0
            while r < n:
                col = c0 + r
                g, off = col // 128, col % 128
                m = min(n - r, 128 - off)
                pieces.append((src[r:r + m, :], g, off, m))
                r += m

        def pf(e, j):
            src, g, off, m = pieces[j]
            return e.dma_start(out=stg[off:off + m, g, :], in_=src)
        P.dma("sp", pf, reads=(), writes=[("ys", 0)], n=len(pieces))
        bk, bkk = bank()
        for g in range(4):
            tr(bk[:, g * 128:(g + 1) * 128], stg[:, g, :], ident[:], [("ys", 0), "ident"], [bkk], last=(g == 3))
        act(lambda e: e.activation(out=PT[:], in_=bk[:, 0:464], func=AF.Copy), [bkk], ["PT"])
        act(lambda e: e.activation(out=small[:, 8:20], in_=PT[:, C_LBL:C_LBL + 12], func=AF.Exp), ["PT"], ["small"])
        dve(lambda e: e.tensor_tensor(out=small[:, 20:24], in0=small[:, 8:12], in1=small[:, 12:16], op=ALU.add), ["small"], ["small"])
        dve(lambda e: e.tensor_tensor(out=small[:, 20:24], in0=small[:, 20:24], in1=small[:, 16:20], op=ALU.add), ["small"], ["small"])
        dve(lambda e: e.reciprocal(out=small[:, 20:24], in_=small[:, 20:24]), ["small"], ["small"])
        dve(lambda e: e.tensor_tensor(out=small[:, 0:4], in0=small[:, 8:12], in1=small[:, 20:24], op=ALU.mult), ["small"], ["small"])
        dve(lambda e: e.tensor_scalar(out=small[:, 4:8], in0=small[:, 0:4], scalar1=-1.0, scalar2=1.0, op0=ALU.mult, op1=ALU.add), ["small"], ["small"])
        act(lambda e: e.activation(out=small[:, 32:36], in_=PT[:, C_RLAM:C_RLAM + 4], func=AF.Exp, scale=-1.0), ["PT", "small"], ["small"])
        act(lambda e: e.activation(out=small[:, 32:36], in_=small[:, 32:36], func=AF.Ln, bias=1.0, scale=1.0), ["small"], ["small"])
        dve(lambda e: e.tensor_scalar(out=small[:, 24:28], in0=small[:, 32:36], scalar1=-8.0, scalar2=None, op0=ALU.mult), ["small"], ["small"])
        dve(lambda e: e.tensor_scalar(out=small[:, 28:32], in0=small[:, 32:36], scalar1=-16.0, scalar2=None, op0=ALU.mult), ["small"], ["small"])

    SM = ["small"]
    def setup2():
        dve(lambda e: e.tensor_scalar(out=small[:, 44:48], in0=PT[:, C_RBA:C_RBA + 4], scalar1=-1.0, scalar2=None, op0=ALU.mult), ["PT", "small"], ["small"])
        dve(lambda e: e.tensor_scalar(out=small[:, 48:52], in0=PT[:, C_RBX:C_RBX + 4], scalar1=-1.0, scalar2=None, op0=ALU.mult), ["PT", "small"], ["small"])


    def load_x(s):
        for tb in range(NTB):
            def ld(e, j, tb=tb):
                return e.dma_start(out=YS[:, 2 * j:2 * j + 2, :].rearrange("p a b -> p (a b)"),
                                   in_=x_d[s, tb * 512 + j * 128: tb * 512 + (j + 1) * 128, :])
            P.dma("sp", ld, reads=(), writes=[("ys", i) for i in range(8)], n=4)
            for c in range(KC):
                bk, bkk = bank()
                for j in range(4):
                    src = YS[:, 2 * j:2 * j + 2, :].rearrange("p a b -> p (a b)")[:, c * 128:(c + 1) * 128]
                    tr(bk[:, j * 128:(j + 1) * 128], src, ident[:], [("ys", 2 * j), ("ys", 2 * j + 1), "ident"], [bkk], last=(j == 3))
                if c % 2 == 0:
                    act(lambda e, bk=bk, c=c, tb=tb: e.activation(out=xT[:, c, tb * 512:(tb + 1) * 512], in_=bk[:], func=AF.Copy), [bkk], [xk(c, tb)])
                else:
                    dve(lambda e, bk=bk, c=c, tb=tb: e.tensor_copy(out=xT[:, c, tb * 512:(tb + 1) * 512], in_=bk[:]), [bkk], [xk(c, tb)])

    def store_x(s):
        for tt in range(16):
            tb = tt // 4
            st = YS[:, 2 * (tt % 2):2 * (tt % 2) + 2, :].rearrange("p a b -> p (a b)")
            stk = [("ys", 2 * (tt % 2)), ("ys", 2 * (tt % 2) + 1)]
            for hf in range(2):
                bk, bkk = bank()
                for cc in range(4):
                    c = hf * 4 + cc
                    tr(bk[:, cc * 128:(cc + 1) * 128], xT[:, c, tt * 128:(tt + 1) * 128], ident[:], [xk(c, tb), "ident"], [bkk], last=(cc == 3))
                if hf == 0:
                    act(lambda e, bk=bk, st=st: e.activation(out=st[:, 0:512], in_=bk[:], func=AF.Copy), [bkk], [stk[0]])
                else:
                    dve(lambda e, bk=bk, st=st: e.tensor_copy(out=st[:, 512:1024], in_=bk[:]), [bkk], [stk[1]])
            P.dma("sp", lambda e, j, st=st, tt=tt: e.dma_start(out=out_d[s, tt * 128:(tt + 1) * 128, :], in_=st), reads=stk, writes=())

    def rstd_from(bk, bkk, dst, dk):
        act(lambda e: e.activation(out=dst, in_=bk[:], func=AF.Ln, bias=EPS, scale=1.0), [bkk], [dk])
        act(lambda e: e.activation(out=dst, in_=dst, func=AF.Exp, scale=-0.5), [dk], [dk])

    def prenorm(gcol, tbs=(0, 1, 2, 3), sqids=(0, 1, 2, 3)):
        for tb in tbs:
            sl = slice(tb * 512, (tb + 1) * 512)
            bk, bkk = bank(hold=True)
            pend = []

            def flush(n_keep):
                while len(pend) > n_keep:
                    c_, sq_, sqk_ = pend.pop(0)
                    mm(bk[:], onesD[:], sq_, c_ == 0, c_ == KC - 1, [sqk_, "onesD"], [bkk], last=True)
            for c in range(KC):
                sq, sqk = tb_(sqids[c % 4])
                if c % 4 == 3:
                    pool(lambda e, sq=sq, c=c: e.tensor_tensor(out=sq, in0=xT[:, c, sl], in1=xT[:, c, sl], op=ALU.mult), [xk(c, tb)], [sqk])
                else:
                    act(lambda e, sq=sq, c=c: e.activation(out=sq, in_=xT[:, c, sl], func=AF.Square), [xk(c, tb)], [sqk])
                pend.append((c, sq, sqk))
                flush(2)
            flush(0)
            rs, rsk = tf(8)
            rstd_from(bk, bkk, rs, rsk)
            release(bkk)
            for c in range(KC):
                dve(lambda e, c=c: e.scalar_tensor_tensor(out=hT[:, c, sl], in0=xT[:, c, sl], scalar=pcol(gcol + c), in1=rs,
                                                         op0=ALU.mult, op1=ALU.mult), [xk(c, tb), rsk, "PT"], [hk(c, tb)])

    def resid(tb, gcol, banks):
        sl = slice(tb * 512, (tb + 1) * 512)
        return sl

    def ys_default(idx, dc):
        return YS[:, dc, :], [("ys", dc)]

    def proj_resid(tbs, gcol, nk, rhs_fn, slab_fn, ysf=ys_default):
        nt = len(tbs)
        sbs = [bank(hold=True) for _ in range(nt)]
        pend = []

        def flush(n_keep):
            while len(pend) > n_keep:
                i_, dc_, sq_, sqk_ = pend.pop(0)
                mm(sbs[i_][0][:], onesD[:], sq_, dc_ == 0, dc_ == KC - 1, [sqk_, "onesD"], [sbs[i_][1]], last=True)
        sqn = [0]
        for dc in range(KC):
            lf, wkey = slab_fn(dc)
            for i_ in range(nt):
                bk, bkk = bank()
                for k in range(nk):
                    ra, rk = rhs_fn(k, i_)
                    mm(bk[:], lf(k), ra, k == 0, k == nk - 1, [wkey, rk], [bkk])
                flush(nt)
                ya, yk = ysf(i_, dc)
                act(lambda e, bk=bk, ya=ya: e.activation(out=ya, in_=bk[:], func=AF.Copy), [bkk], yk)
                sq, sqk = tb_(sqn[0] % 4)
                sqn[0] += 1
                act(lambda e, bk=bk, sq=sq: e.activation(out=sq, in_=bk[:], func=AF.Square), [bkk], [sqk])
                pend.append((i_, dc, sq, sqk))
        flush(0)
        for i_, tb in enumerate(tbs):
            sl = slice(tb * 512, (tb + 1) * 512)
            sbk, sbkk = sbs[i_]
            rs, rsk = tf(8 + (i_ % 2))
            rstd_from(sbk, sbkk, rs, rsk)
            release(sbkk)
            for dc in range(KC):
                ya, yk = ysf(i_, dc)
                if dc % 2 == 0:
                    dve(lambda e, dc=dc, ya=ya, rs=rs: e.scalar_tensor_tensor(out=ya, in0=ya, scalar=pcol(gcol + dc), in1=rs,
                                                                            op0=ALU.mult, op1=ALU.mult), yk + [rsk, "PT"], yk)
                    pool(lambda e, dc=dc, ya=ya, sl=sl: e.tensor_tensor(out=xT[:, dc, sl], in0=xT[:, dc, sl], in1=ya, op=ALU.add),
                         yk + [xk(dc, tb)], [xk(dc, tb)])
                else:
                    dve(lambda e, dc=dc, ya=ya, rs=rs: e.tensor_tensor(out=ya, in0=ya, in1=rs, op=ALU.mult), yk + [rsk], yk)
                    dve(lambda e, dc=dc, ya=ya, sl=sl: e.scalar_tensor_tensor(out=xT[:, dc, sl], in0=ya, scalar=pcol(gcol + dc), in1=xT[:, dc, sl],
                                                                            op0=ALU.mult, op1=ALU.add), yk + [xk(dc, tb), "PT"], [xk(dc, tb)])

    def outproj(w_d, gcol):
        wv = w_d[0]
        plan([[(wv[:, s_ * 256:(s_ + 1) * 256], 8, 0, 256, 256)] for _p in range(2) for s_ in range(4)])
        for pp in range(2):
            slabs = {}
            tbs = [2 * pp, 2 * pp + 1]

            def slab_fn(dc):
                s_ = dc // 2
                if s_ not in slabs:
                    slabs[s_] = next_slab()
                buf, key = slabs[s_]
                v = buf[:, 0:2048].rearrange("p (k w) -> p k w", k=8)
                o = (dc % 2) * 128
                return (lambda k: v[:, k, o:o + 128]), key

            def rhs_fn(k, idx, tbs=tbs):
                return mixT(k, tbs[idx])

            def ysf(idx, dc):
                if idx == 0:
                    return YS[:, dc, :], [("ys", dc)]
                return hT[:, dc, 0:1024].bitcast(F32), [hk(dc, 0), hk(dc, 1)]
            proj_resid(tbs, gcol, KC, rhs_fn, slab_fn, ysf)

    Gf = G[:, 16384:22528].bitcast(F32)

    def gf(i):
        return Gf[:, i * 512:(i + 1) * 512], [("G", 32 + 2 * i), ("G", 33 + 2 * i)]

    def next_slab_np():
        if not issued:
            _issue()
        return issued.pop(0)

    def hgrn_setup(hd):
        bufA, wkeyA = next_slab_np()
        bufB, wkeyB = next_slab_np()
        ctx = dict(hd=hd, WA=bufA[:, 0:2048].rearrange("p (k w) -> p k w", k=8), kA=wkeyA,
                   WB=bufB[:, 0:2048].rearrange("p (k w) -> p k w", k=8), kB=wkeyB)
        pool(lambda e: e.memset(Sst[:], 0.0), [], ["Sst"])
        return ctx

    def fset(p):
        ids = (4, 5, 6, 7) if p == 0 else (1, 9, 10, 11)
        return [tb_(i) for i in ids]

    def hgrn_inproj(ctx, tb, coff):
        W, wkey = (ctx["WA"], ctx["kA"]) if coff < 256 else (ctx["WB"], ctx["kB"])
        co = coff % 256
        sl = slice(tb * 512, (tb + 1) * 512)
        bk, bkk = bank(hold=True)
        for k in range(KC):
            mm(bk[:], W[:, k, co:co + 128], hT[:, k, sl], k == 0, k == KC - 1, [wkey, hk(k, tb)], [bkk])
        return bk, bkk

    def hgrn_front(ctx, tb, p):
        hd = ctx["hd"]
        WB, wkeyB = ctx["WB"], ctx["kB"]
        lbc = small[:, hd:hd + 1]
        omlc = small[:, 4 + hd:5 + hd]
        (qd, kqd), (vt, kvt), (kt_, kkt), (sc, ksc) = fset(p)
        bk, bkk = hgrn_inproj(ctx, tb, 128)
        yield
        t_f, kf = tf(0)
        t_l, kl = tf(1)
        t_b, kb = tf(2)
        t_eb, keb = tf(3)
        t_x, kx = tf(4)
        act(lambda e: e.activation(out=t_f, in_=bk[:], func=AF.Exp, scale=-1.0), [bkk], [kf])
        release(bkk)
        bq, bqk = hgrn_inproj(ctx, tb, 0)
        yield
        act(lambda e: e.activation(out=t_f, in_=t_f, func=AF.Ln, bias=1.0, scale=1.0), [kf], [kf])
        yield
        act(lambda e: e.activation(out=t_f, in_=t_f, func=AF.Exp, scale=-1.0), [kf], [kf])
        yield
        dve(lambda e: e.tensor_scalar(out=t_f, in0=t_f, scalar1=omlc, scalar2=lbc, op0=ALU.mult, op1=ALU.add), [kf] + SM, [kf])
        yield
        act(lambda e: e.activation(out=t_l, in_=t_f, func=AF.Ln), [kf], [kl])
        t_q, kq = tf(5)
        act(lambda e: e.activation(out=t_q, in_=bq[:], func=AF.Exp, scale=-1.0), [bqk], [kq])
        yield
        dve(lambda e: e.tensor_scalar(out=t_f, in0=t_f, scalar1=-1.0, scalar2=1.0, op0=ALU.mult, op1=ALU.add), [kf, kl], [kf])
        dve(lambda e: e.tensor_tensor_scan(out=t_b, data0=cmask[:], data1=t_l, initial=0.0, op0=ALU.mult, op1=ALU.add), [kl, "cmask"], [kb])
        act(lambda e: e.activation(out=t_q, in_=t_q, func=AF.Ln, bias=1.0, scale=1.0), [kq], [kq])
        yield
        act(lambda e: e.activation(out=t_eb, in_=t_b, func=AF.Exp), [kb], [keb])
        act(lambda e: e.activation(out=t_l, in_=t_b, func=AF.Exp, scale=-1.0), [kb], [kl])
        yield
        for n in range(8):
            dve(lambda e, n=n: e.tensor_scalar(out=t_x[:, n * 64:(n + 1) * 64], in0=t_l[:, n * 64:(n + 1) * 64],
                                               scalar1=t_eb[:, n * 64 + 63:n * 64 + 64], scalar2=None, op0=ALU.mult), [kl, keb], [kx])
            if n % 4 == 3:
                yield
        act(lambda e: e.activation(out=t_q, in_=t_q, func=AF.Exp, scale=-1.0), [kq], [kq])
        pool(lambda e: e.tensor_copy(out=decs[:, p, :], in_=t_eb.rearrange("p (a b) -> p a b", b=64)[:, :, 63]), [keb], [("decs", p)])
        kd, kkd = tb_(2)
        ke, kke = tb_(3)
        dve(lambda e: e.tensor_tensor(out=kd, in0=t_f, in1=t_l, op=ALU.mult), [kf, kl], [kkd])
        dve(lambda e: e.tensor_tensor(out=ke, in0=t_f, in1=t_x, op=ALU.mult), [kf, kx], [kke])
        yield
        dve(lambda e: e.tensor_tensor(out=t_q, in0=bq[:], in1=t_q, op=ALU.mult), [bqk, kq], [kq])
        release(bqk)
        yield
        dve(lambda e: e.tensor_tensor(out=qd, in0=t_q, in1=t_eb, op=ALU.mult), [kq, keb], [kqd])
        yield
        bk, bkk = bank(hold=True)
        for j in range(4):
            for k in range(KC):
                mm(bk[:, j * 128:(j + 1) * 128], hT[:, k, tb * 512 + j * 128: tb * 512 + (j + 1) * 128], WB[:, k, 0:128],
                   k == 0, k == KC - 1, [wkeyB, hk(k, tb)], [bkk], last=(k == KC - 1 and j == 3))
        yield
        dve(lambda e: e.tensor_copy(out=vt, in_=bk[:]), [bkk], [kvt])
        release(bkk)
        yield
        for j in range(4):
            tr(pbb[:, j * 128:(j + 1) * 128], ke[:, j * 128:(j + 1) * 128], identb[:], [kke, "identb"], ["pbb"], last=(j == 3))
        dve(lambda e: e.tensor_copy(out=kt_, in_=pbb[:, 0:512]), ["pbb"], [kkt])
        yield
        bk, bkk = bank(hold=True)
        for j in range(4):
            mm(bk[:, j * 128:(j + 1) * 128], kd[:, j * 128:(j + 1) * 128], qd[:, j * 128:(j + 1) * 128], True, True,
               [kkd, kqd], [bkk], last=(j == 3))
        dve(lambda e: e.tensor_tensor(out=sc, in0=bk[:], in1=maskh[:], op=ALU.mult), [bkk, "maskh"], [ksc])
        release(bkk)
        yield

    def hgrn_back(ctx, tb, p):
        hd = ctx["hd"]
        (qd, kqd), (vt, kvt), (kt_, kkt), (sc, ksc) = fset(p)
        dk = ("decs", p)
        S2 = TF[:, 1, 0:128]
        S2k = ("tf", 1)
        stt = [(Sst[:], "Sst"), (S2, S2k)]
        bu0, bu0k = bank(hold=True)
        bu1, bu1k = bank(hold=True)
        for n in (0, 2, 4, 6, 1, 3, 5, 7):
            j, hf = n // 2, n % 2
            bu, buk = (bu0, bu0k) if hf == 0 else (bu1, bu1k)
            rows = slice(hf * 64, (hf + 1) * 64)
            mm(bu[:, j * 128:(j + 1) * 128], kt_[rows, j * 128:(j + 1) * 128], vt[rows, j * 128:(j + 1) * 128], True, True,
               [kkt, kvt], [buk], last=(j == 3))
        yield
        pool(lambda e: e.tensor_copy(out=Sb[:, 0, :], in_=Sst[:]), ["Sst"], ["Sb"])
        for n in range(8):
            bu, buk = (bu0, bu0k) if n % 2 == 0 else (bu1, bu1k)
            (si, sik), (so, sok) = stt[n % 2], stt[(n + 1) % 2]
            dve(lambda e, n=n, bu=bu, si=si, so=so: e.scalar_tensor_tensor(out=so, in0=si, scalar=decs[:, p, n:n + 1],
                                                                         in1=bu[:, (n // 2) * 128:(n // 2 + 1) * 128], op0=ALU.mult, op1=ALU.add),
                [sik, dk, buk], [sok])
            if n < 7:
                pool(lambda e, n=n, so=so: e.tensor_copy(out=Sb[:, n + 1, :], in_=so), [sok], ["Sb"])
            yield
        release(bu0k)
        release(bu1k)
        bg, bgk = hgrn_inproj(ctx, tb, 384)
        yield
        bo, bok = bank(hold=True)
        for j in range(4):
            mm(bo[:, j * 128:(j + 1) * 128], vt[:, j * 128:(j + 1) * 128], sc[:, j * 128:(j + 1) * 128], True, False, [kvt, ksc], [bok], last=False)
            mm(bo[:, j * 128:j * 128 + 64], Sb[:, 2 * j, :], qd[:, j * 128:j * 128 + 64], False, False, ["Sb", kqd], [bok], last=False)
            mm(bo[:, j * 128 + 64:(j + 1) * 128], Sb[:, 2 * j + 1, :], qd[:, j * 128 + 64:(j + 1) * 128], False, True, ["Sb", kqd], [bok], last=(j == 3))
        yield
        osq, kosq = tb_(8)
        act(lambda e: e.activation(out=osq, in_=bo[:], func=AF.Square), [bok], [kosq])
        yield
        t_g, kg = tf(7)
        act(lambda e: e.activation(out=t_g, in_=bg[:], func=AF.Exp, scale=-1.0), [bgk], [kg])
        release(bgk)
        bs, bsk = bank(hold=True)
        mm(bs[:], onesH[:], osq, True, True, [kosq, "onesH"], [bsk])
        yield
        act(lambda e: e.activation(out=t_g, in_=t_g, func=AF.Ln, bias=1.0, scale=1.0), [kg], [kg])
        t_r, kr = tf(6)
        act(lambda e: e.activation(out=t_r, in_=bs[:], func=AF.Ln, bias=EPS, scale=1.0), [bsk], [kr])
        release(bsk)
        yield
        act(lambda e: e.activation(out=t_r, in_=t_r, func=AF.Exp, scale=-0.5), [kr], [kr])
        act(lambda e: e.activation(out=t_g, in_=t_g, func=AF.Exp, scale=-1.0), [kg], [kg])
        yield
        t_o, ko = tf(9)
        dve(lambda e: e.scalar_tensor_tensor(out=t_o, in0=bo[:], scalar=pcol(C_HN + hd), in1=t_r, op0=ALU.mult, op1=ALU.mult),
            [bok, kr, "PT"], [ko])
        release(bok)
        yield
        mo, mok = mixT(hd, tb)
        dve(lambda e: e.tensor_tensor(out=mo, in0=t_o, in1=t_g, op=ALU.mult), [ko, kg], [mok])
        yield

    def rglru_setup(rc):
        buf, wkey = next_slab_np()
        pool(lambda e: e.memset(rhalo[:, rc, :], 0.0), [], ["rhalo"])
        pool(lambda e: e.memset(hlast[:, rc:rc + 1], 0.0), [], ["hlast"])
        return dict(rc=rc, W=buf[:, 0:2048].rearrange("p (k w) -> p k w", k=8), k=wkey)

    def rglru_tb(ctx, tb):
        rc, W, wkey = ctx["rc"], ctx["W"], ctx["k"]
        kxb = ("tf", 2)
        sl = slice(tb * 512, (tb + 1) * 512)
        bk, bkk = bank(hold=True)
        for k in range(KC):
            mm(bk[:], W[:, k, 0:128], hT[:, k, sl], k == 0, k == KC - 1, [wkey, hk(k, tb)], [bkk])
        yield
        pool(lambda e: e.tensor_copy(out=TF[:, 2, 0:3], in_=rhalo[:, rc, :]), ["rhalo"], [kxb])
        act(lambda e: e.activation(out=TF[:, 2, 3:515], in_=bk[:], func=AF.Copy), [bkk], [kxb])
        release(bkk)
        pool(lambda e: e.tensor_copy(out=rhalo[:, rc, :], in_=TF[:, 2, 512:515]), [kxb], ["rhalo"])
        yield
        xf, kxf = gf(0)
        act(lambda e: e.activation(out=xf, in_=TF[:, 2, 3:515], func=AF.Identity, scale=pcol(C_RCW + 3 * 4 + rc), bias=pcol(C_RCB + rc)),
            [kxb, "PT"], kxf)
        yield
        for kk in range(3):
            dve(lambda e, kk=kk: e.scalar_tensor_tensor(out=xf, in0=TF[:, 2, kk:kk + 512], scalar=pcol(C_RCW + kk * 4 + rc), in1=xf,
                                                       op0=ALU.mult, op1=ALU.add), [kxb, "PT"] + kxf, kxf)
            yield
        xfb, kxfb = tb_(0)
        pool(lambda e: e.tensor_copy(out=xfb, in_=xf), kxf, [kxfb])
        yield
        br, brk = bank(hold=True)
        mm(br[:], wabd[:, rc, :], xfb, True, True, [kxfb, "wabd"], [brk])
        bi, bik = bank(hold=True)
        mm(bi[:], wxbd[:, rc, :], xfb, True, True, [kxfb, "wxbd"], [bik])
        yield
        t_r, kr = gf(1)
        t_i, ki = gf(2)
        act(lambda e: e.activation(out=t_r, in_=br[:], func=AF.Exp, scale=-1.0, bias=small[:, 44 + rc:45 + rc]), [brk] + SM, kr)
        release(brk)
        act(lambda e: e.activation(out=t_i, in_=bi[:], func=AF.Exp, scale=-1.0, bias=small[:, 48 + rc:49 + rc]), [bik] + SM, ki)
        release(bik)
        yield
        by, byk = bank(hold=True)
        for k in range(KC):
            mm(by[:], W[:, k, 128:256], hT[:, k, sl], k == 0, k == KC - 1, [wkey, hk(k, tb)], [byk])
        act(lambda e: e.activation(out=t_r, in_=t_r, func=AF.Ln, bias=1.0, scale=1.0), kr, kr)
        act(lambda e: e.activation(out=t_i, in_=t_i, func=AF.Ln, bias=1.0, scale=1.0), ki, ki)
        yield
        act(lambda e: e.activation(out=t_r, in_=t_r, func=AF.Exp, scale=-1.0), kr, kr)
        act(lambda e: e.activation(out=t_i, in_=t_i, func=AF.Exp, scale=-1.0), ki, ki)
        t_g, kg = TF[:, 3, 0:512], ("tf", 3)
        act(lambda e: e.activation(out=t_g, in_=by[:], func=AF.Square), [byk], [kg])
        yield
        t_a, ka = gf(3)
        t_a2, ka2 = gf(4)
        act(lambda e: e.activation(out=t_a, in_=t_r, func=AF.Exp, scale=small[:, 24 + rc:25 + rc]), kr + SM, ka)
        act(lambda e: e.activation(out=t_a2, in_=t_r, func=AF.Exp, scale=small[:, 28 + rc:29 + rc]), kr + SM, ka2)
        dve(lambda e: e.tensor_tensor(out=t_i, in0=t_i, in1=xf, op=ALU.mult), ki + kxf, ki)
        dve(lambda e: e.tensor_scalar(out=t_g, in0=t_g, scalar1=0.044715, scalar2=1.0, op0=ALU.mult, op1=ALU.add), [kg], [kg])
        yield
        dve(lambda e: e.tensor_scalar(out=t_a2, in0=t_a2, scalar1=-1.0, scalar2=1.0, op0=ALU.mult, op1=ALU.add), ka2, ka2)
        dve(lambda e: e.tensor_scalar_max(out=t_a2, in0=t_a2, scalar1=1e-30), ka2, ka2)
        dve(lambda e: e.tensor_tensor(out=t_g, in0=by[:], in1=t_g, op=ALU.mult), [byk, kg], [kg])
        yield
        act(lambda e: e.activation(out=t_a2, in_=t_a2, func=AF.Ln), ka2, ka2)
        act(lambda e: e.activation(out=t_g, in_=t_g, func=AF.Exp, scale=-1.5957691216), [kg], [kg])
        yield
        act(lambda e: e.activation(out=t_a2, in_=t_a2, func=AF.Exp, scale=0.5), ka2, ka2)
        act(lambda e: e.activation(out=t_g, in_=t_g, func=AF.Ln, bias=1.0, scale=1.0), [kg], [kg])
        yield
        dve(lambda e: e.tensor_tensor(out=t_i, in0=t_i, in1=t_a2, op=ALU.mult), ki + ka2, ki)
        act(lambda e: e.activation(out=t_g, in_=t_g, func=AF.Exp, scale=-1.0), [kg], [kg])
        yield
        t_h, kh = gf(5)
        dve(lambda e: e.tensor_tensor_scan(out=t_h, data0=t_a, data1=t_i, initial=hlast[:, rc:rc + 1], op0=ALU.mult, op1=ALU.add),
            ka + ki + ["hlast"], kh)
        dve(lambda e: e.tensor_tensor(out=t_g, in0=by[:], in1=t_g, op=ALU.mult), [byk, kg], [kg])
        release(byk)
        yield
        pool(lambda e: e.tensor_copy(out=hlast[:, rc:rc + 1], in_=t_h[:, 511:512]), kh, ["hlast"])
        mo, mok = mixT(4 + rc, tb)
        dve(lambda e: e.tensor_tensor(out=mo, in0=t_h, in1=t_g, op=ALU.mult), kh + [kg], [mok])
        yield

    def interleave(gens):
        gens = list(gens)
        while gens:
            for g in list(gens):
                try:
                    next(g)
                except StopIteration:
                    gens.remove(g)

    def ffn(l):
        wu = fwu_d[l]
        wd = fwd_d[l]
        gcol = C_G + l * 32 + 3 * 8
        pool(lambda e: e.memset(fhalo[:], 0.0), [("fh", q_) for q_ in range(44)], [("fh", q_) for q_ in range(44)])
        ffn_rr = [0, 0]
        for hb in range(2):
            plan([[(wu[:, j * 128:(j + 1) * 128], 8, 0, 256, 128),
                   (wu[:, DFF + j * 128:DFF + (j + 1) * 128], 8, 128, 256, 128)] for j in range(NJ)])
            plan([[(wd[:, dc * 128:(dc + 1) * 128], NJ, 0, 128, 128)] for dc in range(KC)])
        prefetch()
        prenorm(C_G + l * 32 + 16, (0, 1))
        for hb in range(2):
            steps = [(j, sub) for j in range(NJ) for sub in range(2)]
            st = {}
            Wcur = [None, None]

            def S0(i):
                j, sub = steps[i]
                if sub == 0:
                    buf, wkey = next_slab()
                    Wcur[0] = buf[:, 0:2048].rearrange("p (k w) -> p k w", k=8)
                    Wcur[1] = wkey
                W, wkey = Wcur
                tb = hb * 2 + sub
                sl = slice(tb * 512, (tb + 1) * 512)
                rec = []
                for gv in range(2):
                    jj = gv * NJ + j
                    bk, bkk = bank()
                    for k in range(KC):
                        mm(bk[:], W[:, k, gv * 128:(gv + 1) * 128], hT[:, k, sl], k == 0, k == KC - 1, [wkey, hk(k, tb)], [bkk])
                    ts_ = ffn_rr[0] % 4
                    ffn_rr[0] += 1
                    tc_, ktc = tf(ffn_rr[1] % 8)
                    ffn_rr[1] += 1
                    rec.append((jj, bk, bkk, ts_, ("tf", ts_), ("fh", jj), tc_, ktc, C_FCW + l * 132 + jj))
                for (jj, bk, bkk, ts_, kty, fhk, tc_, ktc, cw) in rec:
                    pool(lambda e, ts_=ts_, jj=jj: e.tensor_copy(out=TF[:, ts_, 0:2], in_=fhalo[:, jj, :]), [fhk], [kty])
                for (jj, bk, bkk, ts_, kty, fhk, tc_, ktc, cw) in rec:
                    act(lambda e, bk=bk, ts_=ts_: e.activation(out=TF[:, ts_, 2:514], in_=bk[:], func=AF.Copy), [bkk], [kty])
                for (jj, bk, bkk, ts_, kty, fhk, tc_, ktc, cw) in rec:
                    pool(lambda e, ts_=ts_, jj=jj: e.tensor_copy(out=fhalo[:, jj, :], in_=TF[:, ts_, 512:514]), [kty], [fhk])
                for (jj, bk, bkk, ts_, kty, fhk, tc_, ktc, cw) in rec:
                    act(lambda e, ts_=ts_, tc_=tc_, cw=cw, jj=jj: e.activation(out=tc_, in_=TF[:, ts_, 2:514], func=AF.Identity,
                                                                            scale=pcol(cw + 88), bias=pcol(C_FCB + l * 44 + jj)),
                        [kty, "PT"], [ktc])
                st[i] = rec

            def S1(i):
                rec = st[i]
                for off_, cofs in ((1, 44), (0, 0)):
                    for (jj, bk, bkk, ts_, kty, fhk, tc_, ktc, cw) in rec:
                        dve(lambda e, ts_=ts_, tc_=tc_, cw=cw, off_=off_, cofs=cofs: e.scalar_tensor_tensor(
                            out=tc_, in0=TF[:, ts_, off_:off_ + 512], scalar=pcol(cw + cofs), in1=tc_, op0=ALU.mult, op1=ALU.add),
                            [kty, ktc, "PT"], [ktc])

            def S2(i):
                j, sub = steps[i]
                rec = st.pop(i)
                ga, gak = rec[0][6], rec[0][7]
                vb, vbk = rec[1][6], rec[1][7]
                act(lambda e: e.activation(out=ga, in_=ga, func=AF.Gelu_apprx_tanh), [gak], [gak])
                go, gok = gvT(j, sub)
                dve(lambda e: e.tensor_tensor(out=go, in0=ga, in1=vb, op=ALU.mult), [gak, vbk], [gok])
            n_ = len(steps)
            for t_ in range(n_ + 2):
                if hb == 0 and t_ == 12:
                    prenorm(C_G + l * 32 + 16, (2, 3))
                if t_ < n_:
                    S0(t_)
                if 0 <= t_ - 1 < n_:
                    S1(t_ - 1)
                if 0 <= t_ - 2 < n_:
                    S2(t_ - 2)
            def slab_fn(dc):
                buf, key = next_slab()
                v = buf[:, 0:NJ * 128].rearrange("p (k w) -> p k w", k=NJ)
                return (lambda k: v[:, k, :]), key

            def rhs_fn(k, idx):
                return gvT(k, idx)

            def ysf(idx, dc, hb=hb):
                if idx == 0:
                    return YS[:, dc, :], [("ys", dc)]
                return hT[:, dc, hb * 1024:(hb + 1) * 1024].bitcast(F32), [hk(dc, 2 * hb), hk(dc, 2 * hb + 1)]
            proj_resid([hb * 2, hb * 2 + 1], gcol, NJ, rhs_fn, slab_fn, ysf)

    def fox_plan():
        wv = owi_d[0]
        for pr_ in range(8):
            plan([[(wv[:, 2048 + pr_ * 128:2048 + (pr_ + 1) * 128], 8, 0, 128, 128)],
                  [(wv[:, pr_ * 128:(pr_ + 1) * 128], 8, 0, 256, 128),
                   (wv[:, 1024 + pr_ * 128:1024 + (pr_ + 1) * 128], 8, 128, 256, 128)]])

    def fox():
        wv = owi_d[0]
        pool(lambda e: e.memset(clast[:], 0.0), [], ["clast"])
        for tb in range(NTB):
            sl = slice(tb * 512, (tb + 1) * 512)
            bk, bkk = bank()
            for k in range(KC):
                mm(bk[0:16, :], wfb[:, k, :], hT[:, k, sl], k == 0, k == KC - 1, ["wfb", hk(k, tb)], [bkk])
            ls = TF[0:16, 0, 0:512]
            act(lambda e, bk=bk: e.activation(out=ls, in_=bk[0:16, :], func=AF.Sigmoid, bias=small[0:16, 40:41]), [bkk, "small_ffb"], [("tf", 0)])
            act(lambda e: e.activation(out=ls, in_=ls, func=AF.Ln), [("tf", 0)], [("tf", 0)])
            dve(lambda e, sl=sl: e.tensor_tensor_scan(out=cT[:, sl], data0=onesF[0:16, :], data1=ls, initial=clast[:, 0:1], op0=ALU.mult, op1=ALU.add),
                [("tf", 0), "onesF", "clast"], [("ys", tb)])
            pool(lambda e, tb=tb: e.tensor_copy(out=clast[:, 0:1], in_=cT[:, tb * 512 + 511:tb * 512 + 512]), [("ys", tb)], ["clast"])
            pool(lambda e, sl=sl: e.tensor_copy(out=crefT[:, sl], in_=cT[:, sl]), [("ys", tb)], [("G", 40 + tb)])
        bk, bkk = bank()
        for kt in range(16):
            tr(bk[:, kt * 16:(kt + 1) * 16], cT[:, kt * 128:(kt + 1) * 128], ident[0:16, 0:16], [("ys", kt // 4), "ident"], [bkk], last=(kt == 15))
        act(lambda e, bk=bk: e.activation(out=negc[:].rearrange("p a b -> p (a b)"), in_=bk[:, 0:256], func=AF.Copy, scale=-1.0), [bkk], ["negc"])
        pool(lambda e: e.memset(vaug[:, :, 0, 64:128], 1.0), [bkk], VK)
        pool(lambda e: e.memset(vaug[:, :, 1, 0:64], 1.0), [], VK)
        vaugB = YSb[:, 0:4096].rearrange("p (t e c) -> p t e c", t=16, e=2)
        VKB = [("ys", i) for i in range(0, 4)]
        pool(lambda e: e.memset(vaugB[:, :, 0, 64:128], 1.0), [], VKB)
        pool(lambda e: e.memset(vaugB[:, :, 1, 0:64], 1.0), [], VKB)
        VA, VKS = [vaug, vaugB], [VK, VKB]

        def vproj(pr_):
            va, vk = VA[pr_ % 2], VKS[pr_ % 2]
            bufV, wkeyV = next_slab()
            WV = bufV[:, 0:1024].rearrange("p (k w) -> p k w", k=8)
            for g4 in range(4):
                bk, bkk = bank(hold=True)
                for j in range(4):
                    tt = g4 * 4 + j
                    for k in range(KC):
                        mm(bk[:, j * 128:(j + 1) * 128], hT[:, k, tt * 128:(tt + 1) * 128], WV[:, k, 0:128], k == 0, k == KC - 1,
                           [wkeyV, hk(k, g4)], [bkk], last=(k == KC - 1 and j == 3))
                bv = bk[:].rearrange("p (j e c) -> p j e c", j=4, e=2)
                dve(lambda e, bv=bv, g4=g4, va=va: e.tensor_copy(out=va[:, g4 * 4:(g4 + 1) * 4, 0, 0:64], in_=bv[:, :, 0, :]), [bkk], vk)
                dve(lambda e, bv=bv, g4=g4, va=va: e.tensor_copy(out=va[:, g4 * 4:(g4 + 1) * 4, 1, 64:128], in_=bv[:, :, 1, :]), [bkk], vk)
                release(bkk)
        vproj(0)
        pool(lambda e: e.memset(kaug[64:65, :], 1.0), [], KK)
        qaug2 = TB_[:, 4:8, :].rearrange("p a b -> p (a b)")
        kaug2 = TB_[:, 8:12, :].rearrange("p a b -> p (a b)")
        QK2 = [("tbf", 4 + t) for t in range(4)]
        KK2 = [("tbf", 8 + t) for t in range(4)]
        pool(lambda e: e.memset(kaug2[64:65, :], 1.0), [], KK2)
        QA, KA, QKS, KKS = [qaug, qaug2], [kaug, kaug2], [QK, QK2], [KK, KK2]
        for pr in range(8):
            buf, wkey = next_slab()
            W = buf[:, 0:2048].rearrange("p (k w) -> p k w", k=8)
            for tb in range(NTB):
                sl = slice(tb * 512, (tb + 1) * 512)
                bq, bqk = bank()
                for k in range(KC):
                    mm(bq[:], W[:, k, 0:128], hT[:, k, sl], k == 0, k == KC - 1, [wkey, hk(k, tb)], [bqk])
                act(lambda e, bq=bq, sl=sl: e.activation(out=QA[0][0:64, sl], in_=bq[0:64, :], func=AF.Copy, scale=0.125), [bqk], [QKS[0][tb]])
                act(lambda e, bq=bq, sl=sl: e.activation(out=QA[1][0:64, sl], in_=bq[64:128, :], func=AF.Copy, scale=0.125), [bqk], [QKS[1][tb]])
                bkx, bkxk = bank()
                for k in range(KC):
                    mm(bkx[:], W[:, k, 128:256], hT[:, k, sl], k == 0, k == KC - 1, [wkey, hk(k, tb)], [bkxk])
                dve(lambda e, bkx=bkx, sl=sl: e.tensor_copy(out=KA[0][0:64, sl], in_=bkx[0:64, :]), [bkxk], [KKS[0][tb]])
                dve(lambda e, bkx=bkx, sl=sl: e.tensor_copy(out=KA[1][0:64, sl], in_=bkx[64:128, :]), [bkxk], [KKS[1][tb]])
            for e2 in range(2):
                h_ = pr * 2 + e2
                P.dma("sp", lambda e, j, e2=e2, h_=h_: e.dma_start(out=QA[e2][64:65, :], in_=crefT[h_:h_ + 1, :]),
                      reads=[("G", 40 + t) for t in range(4)], writes=QKS[e2])
            def kt_order(qb):
                n = 4 * qb + 4
                if qb == 0:
                    return list(range(4))
                return [0] + list(range(4 * qb, n)) + list(range(1, 4 * qb))
            its = [(e2, qb, kt) for e2 in range(2) for qb in range(4) for kt in kt_order(qb)]
            LA = 3
            sb_ = {}
            bos = {}
            for i in range(len(its) + LA):
                if i < len(its):
                    e2, qb, kt = its[i]
                    qaug_, kaug_, QK_, KK_ = QA[e2], KA[e2], QKS[e2], KKS[e2]
                    j = kt - 4 * qb
                    off = 128 * j if j > 0 else 0
                    N = 512 - off
                    bs, bsk = bank(hold=True)
                    mm(bs[:, 0:N], kaug_[0:65, kt * 128:(kt + 1) * 128], qaug_[0:65, qb * 512 + off:(qb + 1) * 512], True, True,
                       [KK_[kt // 4], QK_[qb]], [bsk])
                    sb_[i] = (bs, bsk, off, N, j)
                if i == 24 and pr + 1 < 8:
                    vproj(pr + 1)
                i2 = i - LA
                if i2 >= 0:
                    e2, qb, kt = its[i2]
                    h = pr * 2 + e2
                    order = kt_order(qb)
                    first, last_ = (kt == order[0]), (kt == order[-1])
                    bs, bsk, off, N, j = sb_.pop(i2)
                    if first:
                        bos[(e2, qb)] = bank(hold=True)
                    bo, bok = bos[(e2, qb)]
                    pt, ptk = tb_(i2 % 4)
                    act(lambda e, bs=bs, pt=pt, N=N, kt=kt, h=h: e.activation(out=pt[:, 0:N], in_=bs[:, 0:N], func=AF.Exp, bias=negc[:, kt, h:h + 1]),
                        [bsk, "negc"], [ptk])
                    release(bsk)
                    if j >= 0:
                        pool(lambda e, pt=pt: e.tensor_tensor(out=pt[:, 0:128], in0=pt[:, 0:128], in1=tri[:], op=ALU.mult), [ptk, "tri"], [ptk])
                    mm(bo[:, off:512], VA[pr % 2][:, kt, e2, :], pt[:, 0:N], first, last_, VKS[pr % 2] + [ptk], [bok], last=True)
                    if last_:
                        rr, rrk = tf(8 + (qb % 2))
                        cp, cpk = TF[:, 2 + (qb % 2), 0:512], ("tf", 2 + (qb % 2))
                        mo, mok = mixT(pr, qb)
                        dve(lambda e, bo=bo, cp=cp: e.tensor_copy(out=cp, in_=bo[:]), [bok], [cpk])
                        release(bok)
                        if e2 == 0:
                            dve(lambda e, cp=cp, rr=rr: e.reciprocal(out=rr[0:64, :], in_=cp[64:128, :]), [cpk], [rrk])
                            dve(lambda e, cp=cp, mo=mo, rr=rr: e.tensor_tensor(out=mo[0:64, :], in0=cp[0:64, :], in1=rr[0:64, :], op=ALU.mult), [cpk, rrk], [mok])
                        else:
                            dve(lambda e, cp=cp, rr=rr: e.reciprocal(out=rr[64:128, :], in_=cp[0:64, :]), [cpk], [rrk])
                            dve(lambda e, cp=cp, mo=mo, rr=rr: e.tensor_tensor(out=mo[64:128, :], in0=cp[64:128, :], in1=rr[64:128, :], op=ALU.mult), [cpk, rrk], [mok])

    setup()
    setup2()
    for s in range(nseq):
        load_x(s)
        import os as _os
        parts = _os.environ.get("KPARTS", "hgrn,rglru,outproj").split(",")
        if nph >= 1:
            prenorm(C_G + 0, (0,))
            wv0 = ewi_d[0]
            hspecA = lambda hd: [(wv0[:, c0:c0 + 128], 8, j * 128, 256, 128) for j, c0 in enumerate([hd * 128, 512 + hd * 128])]
            hspecB = lambda hd: [(wv0[:, c0:c0 + 128], 8, j * 128, 256, 128) for j, c0 in enumerate([1024 + hd * 128, 1536 + hd * 128])]
            rspec = lambda rc: [(wv0[:, 2048 + rc * 128:2048 + (rc + 1) * 128], 8, 0, 256, 128),
                                (wv0[:, 2560 + rc * 128:2560 + (rc + 1) * 128], 8, 128, 256, 128)]
            for i_ in range(4):
                plan([hspecA(i_), hspecB(i_), rspec(i_)])
                ch = hgrn_setup(i_)
                cr = rglru_setup(i_)
                for st_ in range(5):
                    gens = []
                    if st_ >= 1:
                        gens.append(hgrn_back(ch, st_ - 1, (st_ - 1) % 2))
                    if st_ < 4:
                        gens.append(hgrn_front(ch, st_, st_ % 2))
                        gens.append(rglru_tb(cr, st_))
                    if i_ == 0 and st_ < 3:
                        prenorm(C_G + 0, (st_ + 1,), (0, 2, 3, 8))
                    interleave(gens)
            if "outproj" in parts:
                outproj(ewo_d, C_G + 8)
        if nph >= 2:
            ffn(0)
        if nph >= 3:
            fox_plan()
            prefetch()
            prenorm(C_G + 32)
            fox()
            outproj(owo_d, C_G + 32 + 8)
        if nph >= 4:
            ffn(1)
        store_x(s)
    P.wait_all("sp")
    P.build()
    return nc, P


_CACHE = {}


def kernel(**inputs):
    if "nc" not in _CACHE:
        _CACHE["nc"] = build_nc()[0]
    nc = _CACHE["nc"]
    x = np.ascontiguousarray(inputs["x"], dtype=np.float32)
    in_maps = []
    for c in range(NCORES):
        m = {k: np.ascontiguousarray(v, dtype=np.float32) for k, v in inputs.items() if k != "x"}
        m["x"] = np.ascontiguousarray(x[c * SPC:(c + 1) * SPC])
        in_maps.append(m)
    res = run_bass_kernel_spmd(nc, in_maps, core_ids=list(range(NCORES)))
    out = np.concatenate([np.asarray(r["out"]) for r in res.results], axis=0)
    return out.astype(np.float32)
```

```python
import numpy as np
import concourse.bass as bass
import concourse.mybir as mybir
from concourse.bass_utils import run_bass_kernel_spmd

F32 = mybir.dt.float32
BF16 = mybir.dt.bfloat16
AF = mybir.ActivationFunctionType
ALU = mybir.AluOpType
AX = mybir.AxisListType

ENGS = ("pe", "act", "dve", "pool", "sp")


class Prog:
    def __init__(self, nc, n_dma_sems=24):
        self.nc = nc
        self.streams = {e: [] for e in ENGS}
        self.eobj = {"pe": nc.tensor, "act": nc.scalar, "dve": nc.vector, "pool": nc.gpsimd, "sp": nc.sync}
        self.sem = {}
        self.tick = {}
        self._ctx = []
        for e in ENGS:
            g = nc.semaphore("s_" + e)
            self.sem[e] = g.__enter__()
            self._ctx.append(g)
            self.tick[e] = 0
        self.dma_sems = []
        for i in range(n_dma_sems):
            g = nc.semaphore("s_dma%d" % i)
            self.dma_sems.append(g.__enter__())
            self._ctx.append(g)
        self.dma_val = [0] * n_dma_sems
        self.dma_rr = 0
        self.dma_rr_q = {}
        self.semobj = {}
        for e in ENGS:
            self.semobj[("e", e)] = self.sem[e]
        for i, s in enumerate(self.dma_sems):
            self.semobj[("d", i)] = s
        self.seen = {e: {} for e in ENGS}
        self.lastw = {}
        self.reads = {}
        self.ninst = 0

    def _deps(self, reads, writes):
        deps = {}

        def add(sk, v):
            if deps.get(sk, 0) < v:
                deps[sk] = v
        for k in reads:
            w = self.lastw.get(k)
            if w:
                add(*w)
        for k in writes:
            w = self.lastw.get(k)
            if w:
                add(*w)
            for sk, v in self.reads.get(k, {}).items():
                add(sk, v)
        return deps

    def _record(self, reads, writes, sk, val):
        for k in writes:
            self.lastw[k] = (sk, val)
            self.reads[k] = {}
        for k in reads:
            d = self.reads.setdefault(k, {})
            if d.get(sk, 0) < val:
                d[sk] = val

    def _emit_waits(self, eng, deps):
        seen = self.seen[eng]
        for sk, v in deps.items():
            if seen.get(sk, 0) >= v:
                continue
            if sk == ("e", eng) and (eng == "pe" or v > self.tick[eng]):
                continue
            seen[sk] = v
            so = self.semobj[sk]
            self.streams[eng].append(("wait", so, v))
            self.eobj[eng].wait_ge(so, v)

    def op(self, eng, fn, reads=(), writes=(), inc=True):
        deps = self._deps(reads, writes)
        self._emit_waits(eng, deps)
        if inc:
            self.tick[eng] += 1
            val = self.tick[eng]
            self.streams[eng].append(("op", None, self.sem[eng], 1))
            fn(self.eobj[eng]).then_inc(self.sem[eng], 1)
        else:
            val = self.tick[eng] + 1
            self.streams[eng].append(("op", None, None, 0))
            fn(self.eobj[eng])
        self._record(reads, writes, ("e", eng), val)
        self.ninst += 1
        return val

    def dma(self, q, fn, reads=(), writes=(), n=1, slot=None):
        if slot is None:
            half = len(self.dma_sems) // 2
            base = 0 if q == "pool" else half
            rr = self.dma_rr_q.get(q, 0)
            slot = base + rr
            self.dma_rr_q[q] = (rr + 1) % half
        sk = ("d", slot)
        deps = self._deps(reads, writes)
        prev = self.dma_val[slot]
        if prev:
            if deps.get(sk, 0) < prev:
                deps[sk] = prev
        self._emit_waits(q, deps)
        so = self.dma_sems[slot]
        for i in range(n):
            self.streams[q].append(("dma", None, i, so))
            fn(self.eobj[q], i).then_inc(so, 16)
        self.dma_val[slot] = prev + 16 * n
        val = self.dma_val[slot]
        self._record(reads, writes, sk, val)
        self.ninst += n
        return sk, val

    def wait_all(self, eng):
        deps = {}
        for e in ENGS:
            if self.tick[e]:
                deps[("e", e)] = self.tick[e]
        for i, v in enumerate(self.dma_val):
            if v:
                deps[("d", i)] = v
        self._emit_waits(eng, deps)

    def build(self):
        pass

    def close(self):
        for g in reversed(self._ctx):
            g.__exit__(None, None, None)

from contextlib import ExitStack

NCORES = 8
SPC = 2
T = 2048
D = 1024
KC = 8
NTB = 4
TBW = 512
DFF = 2816
NJ = 22
EPS = 1e-6

C_G = 0
C_LBL = 64
C_HN = 76
C_RCW = 80
C_RCB = 96
C_RBA = 100
C_RBX = 104
C_RLAM = 108
C_FCW = 112
C_FCB = 376
C_END = 464


def build_nc(nph=4, nseq=SPC):
    nc = bass.Bass("TRN2", target_bir_lowering=False)
    es = ExitStack()

    def din(name, shape):
        return nc.dram_tensor(name, list(shape), F32, kind="ExternalInput").ap()

    x_d = din("x", [SPC, T, D])
    ng_d = din("norm_gains", [2, 4, 1024])
    ewi_d = din("even_w_in", [1, 1024, 3072])
    lbl_d = din("hgrn_lb_logits", [3, 512])
    hn_d = din("hgrn_norm", [1, 512])
    rcw_d = din("rg_conv_w", [1, 4, 512])
    rcb_d = din("rg_conv_b", [1, 512])
    rwa_d = din("rg_wa", [1, 8, 64, 64])
    rba_d = din("rg_ba", [1, 512])
    rwx_d = din("rg_wx", [1, 8, 64, 64])
    rbx_d = din("rg_bx", [1, 512])
    rlam_d = din("rg_lambda", [1, 512])
    ewo_d = din("even_w_out", [1, 1024, 1024])
    owi_d = din("odd_w_in", [1, 1024, 3088])
    ffb_d = din("fox_f_bias", [1, 16])
    owo_d = din("odd_w_out", [1, 1024, 1024])
    fwu_d = din("ffn_w_up", [2, 1024, 5632])
    fcw_d = din("ffn_conv_w", [2, 3, 5632])
    fcb_d = din("ffn_conv_b", [2, 5632])
    fwd_d = din("ffn_w_down", [2, 2816, 1024])
    out_d = nc.dram_tensor("out", [SPC, T, D], F32, kind="ExternalOutput").ap()

    P = Prog(nc)

    def sb(name, shape, dt):
        return es.enter_context(nc.sbuf_tensor(name, list(shape), dt))

    def psum(name, shape, dt):
        return es.enter_context(nc.psum_tensor(name, list(shape), dt))

    xT = sb("xT", [128, KC, T], F32)
    hT = sb("hT", [128, KC, T], BF16)
    G = sb("G", [128, 22528], BF16)
    WS = [sb("ws%d" % i, [128, 2816], BF16) for i in range(3)]
    YS = sb("YS", [128, 8, 512], F32)
    TF = sb("TF", [128, 4, 516], F32)
    TB_ = sb("TB", [128, 12, 512], BF16)
    PT = sb("PT", [128, 464], F32)
    ident = sb("ident", [128, 128], F32)
    identb = sb("identb", [128, 128], BF16)
    onesD = sb("onesD", [128, 128], BF16)
    onesH = sb("onesH", [128, 128], BF16)
    maskh = sb("maskh", [128, 512], BF16)
    tri = sb("tri", [128, 128], BF16)
    cmask = sb("cmask", [128, 512], BF16)
    onesF = sb("onesF", [128, 512], BF16)
    wabd = sb("wabd", [128, 4, 128], BF16)
    wxbd = sb("wxbd", [128, 4, 128], BF16)
    wfb = sb("wfb", [128, KC, 16], BF16)
    selc = sb("selc", [16, 17, 65], BF16)
    small = sb("small", [128, 64], F32)
    Sst = sb("Sst", [128, 128], F32)
    decs = sb("decs", [128, 2, 8], F32)
    Sb = sb("Sb", [128, 9, 128], BF16)
    hlast = sb("hlast", [128, 4], F32)
    fhalo = sb("fhalo", [128, 44, 2], F32)
    rhalo = sb("rhalo", [128, 4, 3], F32)
    negc = sb("negc", [128, 16, 16], F32)
    clast = sb("clast", [16, 1], F32)

    pb = [psum("pb%d" % i, [128, 512], F32) for i in range(7)]
    pbb = psum("pbb", [128, 1024], BF16)
    bank_rr = [0]

    held = set()

    def bank(hold=False):
        for _ in range(8):
            i = bank_rr[0]
            bank_rr[0] = (i + 1) % 7
            if i not in held:
                break
        else:
            raise RuntimeError("no free psum bank")
        if hold:
            held.add(i)
        return pb[i], ("pb", i)

    def release(key):
        held.discard(key[1])

    def mixT(c, tb):
        return G[:, c * 2048 + tb * 512: c * 2048 + (tb + 1) * 512], ("G", 4 * c + tb)

    def gvT(kc, sub):
        return G[:, kc * 1024 + sub * 512: kc * 1024 + (sub + 1) * 512], ("G", 2 * kc + sub)

    qaug = G[:, 16384:18432]
    kaug = G[:, 18432:20480]
    crefT = G[0:16, 20480:22528]
    QK = [("G", g) for g in range(32, 36)]
    KK = [("G", g) for g in range(36, 40)]
    YSb = YS[:].rearrange("p a b -> p (a b)").bitcast(BF16)
    vaug = YSb[:, 4096:8192].rearrange("p (t e c) -> p t e c", t=16, e=2)
    VK = [("ys", i) for i in range(4, 8)]
    cT = YS[:].rearrange("p a b -> p (a b)")[0:16, 0:2048]

    def tf(i):
        if i < 8:
            return YS[:, i, :], ("ys", i)
        return TF[:, i - 8, 0:512], ("tf", i - 8)

    def tb_(i):
        return TB_[:, i, :], ("tbf", i)

    def xk(c, tb):
        return ("x", c, tb)

    def hk(c, tb):
        return ("h", c, tb)

    def pcol(c):
        return PT[:, c:c + 1]

    act = lambda fn, r, w: P.op("act", fn, r, w)
    dve = lambda fn, r, w: P.op("dve", fn, r, w)
    pool = lambda fn, r, w: P.op("pool", fn, r, w)

    def mm(out, lhsT, rhs, start, stop, r, w, last=None):
        if last is None:
            last = stop
        P.op("pe", lambda e: e.matmul(out, lhsT=lhsT, rhs=rhs, start=start, stop=stop), r, w, inc=last)

    def tr(out, in_, idn, r, w, last=True):
        P.op("pe", lambda e: e.transpose(out=out, in_=in_, identity=idn), r, w, inc=last)

    ws_n = [0]
    plan_q = []
    issued = []

    def plan(specs):
        plan_q.extend(specs)

    def _issue():
        parts = plan_q.pop(0)
        i = ws_n[0] % 3
        ws_n[0] += 1
        buf = WS[i]
        key = ("ws", i)

        def fn(e, j):
            src, kcn, off, W, ncols = parts[j]
            dst = buf[:, 0:kcn * W].rearrange("p (k w) -> p k w", k=kcn)[:, :, off:off + ncols]
            return e.dma_start(out=dst, in_=src.rearrange("(k p) n -> p k n", p=128))
        P.dma("pool", fn, reads=(), writes=[key], n=len(parts))
        issued.append((buf, key))

    def prefetch():
        while len(issued) < 3 and plan_q:
            _issue()

    def next_slab():
        while len(issued) < 3 and plan_q:
            _issue()
        return issued.pop(0)

    def setup():
        pool(lambda e: e.memset(ident[:], 0.0), [], ["ident"])
        pool(lambda e: e.affine_select(out=ident[:], in_=ident[:], pattern=[[-1, 128]], compare_op=ALU.not_equal,
                                       fill=1.0, base=0, channel_multiplier=1), ["ident"], ["ident"])
        pool(lambda e: e.tensor_copy(out=identb[:], in_=ident[:]), ["ident"], ["identb"])
        pool(lambda e: e.memset(onesD[:], 1.0 / 1024.0), [], ["onesD"])
        pool(lambda e: e.memset(onesH[:], 1.0 / 128.0), [], ["onesH"])
        pool(lambda e: e.memset(onesF[:], 1.0), [], ["onesF"])
        pool(lambda e: e.memset(tri[:], 1.0), [], ["tri"])
        pool(lambda e: e.affine_select(out=tri[:], in_=tri[:], pattern=[[1, 128]], compare_op=ALU.is_ge,
                                       fill=0.0, base=0, channel_multiplier=-1), ["tri"], ["tri"])
        pool(lambda e: e.memset(maskh[:], 1.0), [], ["maskh"])
        for r in range(4):
            pool(lambda e, r=r: e.affine_select(out=maskh[:, r * 128:(r + 1) * 128], in_=maskh[:, r * 128:(r + 1) * 128],
                                                pattern=[[1, 128]], compare_op=ALU.is_ge, fill=0.0, base=0,
                                                channel_multiplier=-1), ["maskh"], ["maskh"])
            pool(lambda e, r=r: e.memset(maskh[0:64, r * 128 + 64:(r + 1) * 128], 0.0), ["maskh"], ["maskh"])
        pool(lambda e: e.memset(cmask[:], 1.0), [], ["cmask"])
        pool(lambda e: e.memset(cmask[:].rearrange("p (a b) -> p a b", b=64)[:, :, 0:1], 0.0), ["cmask"], ["cmask"])
        pool(lambda e: e.memset(selc[:], 0.0), [], ["selc"])
        pool(lambda e: e.affine_select(out=selc[:, 0:16, 64], in_=selc[:, 0:16, 64], pattern=[[-1, 16]],
                                       compare_op=ALU.not_equal, fill=8.0, base=0, channel_multiplier=1),
             ["selc"], ["selc"])
        pool(lambda e: e.memset(wabd[:], 0.0), [], ["wabd"])
        pool(lambda e: e.memset(wxbd[:], 0.0), [], ["wxbd"])

        def bd(e, j):
            which, rc, blk = j // 8, (j % 8) // 2, j % 2
            dst = (wabd if which == 0 else wxbd)[blk * 64:(blk + 1) * 64, rc, blk * 64:(blk + 1) * 64]
            src = (rwa_d if which == 0 else rwx_d)[0, 2 * rc + blk]
            return e.dma_start(out=dst, in_=src)
        P.dma("pool", bd, reads=(), writes=["wabd", "wxbd"], n=16)
        P.dma("pool", lambda e, j: e.dma_start(out=wfb[:], in_=owi_d[0, :, 3072:3088].rearrange("(k p) n -> p k n", p=128)),
              reads=(), writes=["wfb"])
        P.dma("sp", lambda e, j: e.dma_start(out=small[0:16, 40:41], in_=ffb_d[0].rearrange("(h a) -> h a", a=1)), reads=(), writes=["small_ffb"])
        stg = YS[:, 0, :].rearrange("p (g c) -> p g c", g=4)
        pool(lambda e: e.memset(YS[:, 0, :], 0.0), [], [("ys", 0)])
        rows = [
            (ng_d.rearrange("l j (c p) -> (l j c) p", p=128), C_G, 64),
            (lbl_d.rearrange("l (c p) -> (l c) p", p=128), C_LBL, 12),
            (hn_d.rearrange("l (c p) -> (l c) p", p=128), C_HN, 4),
            (rcw_d.rearrange("l k (c p) -> (l k c) p", p=128), C_RCW, 16),
            (rcb_d.rearrange("l (c p) -> (l c) p", p=128), C_RCB, 4),
            (rba_d.rearrange("l (c p) -> (l c) p", p=128), C_RBA, 4),
            (rbx_d.rearrange("l (c p) -> (l c) p", p=128), C_RBX, 4),
            (rlam_d.rearrange("l (c p) -> (l c) p", p=128), C_RLAM, 4),
            (fcw_d.rearrange("l k (c p) -> (l k c) p", p=128), C_FCW, 264),
            (fcb_d.rearrange("l (c p) -> (l c) p", p=128), C_FCB, 88),
        ]
        pieces = []
        for src, c0, n in rows:
            r = 0
            while r < n:
                col = c0 + r
                g, off = col // 128, col % 128
                m = min(n - r, 128 - off)
                pieces.append((src[r:r + m, :], g, off, m))
                r += m

        def pf(e, j):
            src, g, off, m = pieces[j]
            return e.dma_start(out=stg[off:off + m, g, :], in_=src)
        P.dma("sp", pf, reads=(), writes=[("ys", 0)], n=len(pieces))
        bk, bkk = bank()
        for g in range(4):
            tr(bk[:, g * 128:(g + 1) * 128], stg[:, g, :], ident[:], [("ys", 0), "ident"], [bkk], last=(g == 3))
        act(lambda e: e.activation(out=PT[:], in_=bk[:, 0:464], func=AF.Copy), [bkk], ["PT"])
        act(lambda e: e.activation(out=small[:, 8:20], in_=PT[:, C_LBL:C_LBL + 12], func=AF.Exp), ["PT"], ["small"])
        dve(lambda e: e.tensor_tensor(out=small[:, 20:24], in0=small[:, 8:12], in1=small[:, 12:16], op=ALU.add), ["small"], ["small"])
        dve(lambda e: e.tensor_tensor(out=small[:, 20:24], in0=small[:, 20:24], in1=small[:, 16:20], op=ALU.add), ["small"], ["small"])
        dve(lambda e: e.reciprocal(out=small[:, 20:24], in_=small[:, 20:24]), ["small"], ["small"])
        dve(lambda e: e.tensor_tensor(out=small[:, 0:4], in0=small[:, 8:12], in1=small[:, 20:24], op=ALU.mult), ["small"], ["small"])
        dve(lambda e: e.tensor_scalar(out=small[:, 4:8], in0=small[:, 0:4], scalar1=-1.0, scalar2=1.0, op0=ALU.mult, op1=ALU.add), ["small"], ["small"])
        act(lambda e: e.activation(out=small[:, 32:36], in_=PT[:, C_RLAM:C_RLAM + 4], func=AF.Exp, scale=-1.0), ["PT", "small"], ["small"])
        act(lambda e: e.activation(out=small[:, 32:36], in_=small[:, 32:36], func=AF.Ln, bias=1.0, scale=1.0), ["small"], ["small"])
        dve(lambda e: e.tensor_scalar(out=small[:, 24:28], in0=small[:, 32:36], scalar1=-8.0, scalar2=None, op0=ALU.mult), ["small"], ["small"])
        dve(lambda e: e.tensor_scalar(out=small[:, 28:32], in0=small[:, 32:36], scalar1=-16.0, scalar2=None, op0=ALU.mult), ["small"], ["small"])

    SM = ["small"]
    def setup2():
        dve(lambda e: e.tensor_scalar(out=small[:, 44:48], in0=PT[:, C_RBA:C_RBA + 4], scalar1=-1.0, scalar2=None, op0=ALU.mult), ["PT", "small"], ["small"])
        dve(lambda e: e.tensor_scalar(out=small[:, 48:52], in0=PT[:, C_RBX:C_RBX + 4], scalar1=-1.0, scalar2=None, op0=ALU.mult), ["PT", "small"], ["small"])


    def load_x(s):
        for tb in range(NTB):
            def ld(e, j, tb=tb):
                return e.dma_start(out=YS[:, 2 * j:2 * j + 2, :].rearrange("p a b -> p (a b)"),
                                   in_=x_d[s, tb * 512 + j * 128: tb * 512 + (j + 1) * 128, :])
            P.dma("sp", ld, reads=(), writes=[("ys", i) for i in range(8)], n=4)
            for c in range(KC):
                bk, bkk = bank()
                for j in range(4):
                    src = YS[:, 2 * j:2 * j + 2, :].rearrange("p a b -> p (a b)")[:, c * 128:(c + 1) * 128]
                    tr(bk[:, j * 128:(j + 1) * 128], src, ident[:], [("ys", 2 * j), ("ys", 2 * j + 1), "ident"], [bkk], last=(j == 3))
                if c % 2 == 0:
                    act(lambda e, bk=bk, c=c, tb=tb: e.activation(out=xT[:, c, tb * 512:(tb + 1) * 512], in_=bk[:], func=AF.Copy), [bkk], [xk(c, tb)])
                else:
                    dve(lambda e, bk=bk, c=c, tb=tb: e.tensor_copy(out=xT[:, c, tb * 512:(tb + 1) * 512], in_=bk[:]), [bkk], [xk(c, tb)])

    def store_x(s):
        for tt in range(16):
            tb = tt // 4
            st = YS[:, 2 * (tt % 2):2 * (tt % 2) + 2, :].rearrange("p a b -> p (a b)")
            stk = [("ys", 2 * (tt % 2)), ("ys", 2 * (tt % 2) + 1)]
            for hf in range(2):
                bk, bkk = bank()
                for cc in range(4):
                    c = hf * 4 + cc
                    tr(bk[:, cc * 128:(cc + 1) * 128], xT[:, c, tt * 128:(tt + 1) * 128], ident[:], [xk(c, tb), "ident"], [bkk], last=(cc == 3))
                if hf == 0:
                    act(lambda e, bk=bk, st=st: e.activation(out=st[:, 0:512], in_=bk[:], func=AF.Copy), [bkk], [stk[0]])
                else:
                    dve(lambda e, bk=bk, st=st: e.tensor_copy(out=st[:, 512:1024], in_=bk[:]), [bkk], [stk[1]])
            P.dma("sp", lambda e, j, st=st, tt=tt: e.dma_start(out=out_d[s, tt * 128:(tt + 1) * 128, :], in_=st), reads=stk, writes=())

    def rstd_from(bk, bkk, dst, dk):
        act(lambda e: e.activation(out=dst, in_=bk[:], func=AF.Ln, bias=EPS, scale=1.0), [bkk], [dk])
        act(lambda e: e.activation(out=dst, in_=dst, func=AF.Exp, scale=-0.5), [dk], [dk])

    def prenorm(gcol, tbs=(0, 1, 2, 3), sqids=(0, 1, 2, 3)):
        for tb in tbs:
            sl = slice(tb * 512, (tb + 1) * 512)
            bk, bkk = bank(hold=True)
            pend = []

            def flush(n_keep):
                while len(pend) > n_keep:
                    c_, sq_, sqk_ = pend.pop(0)
                    mm(bk[:], onesD[:], sq_, c_ == 0, c_ == KC - 1, [sqk_, "onesD"], [bkk], last=True)
            for c in range(KC):
                sq, sqk = tb_(sqids[c % 4])
                if c % 4 == 3:
                    pool(lambda e, sq=sq, c=c: e.tensor_tensor(out=sq, in0=xT[:, c, sl], in1=xT[:, c, sl], op=ALU.mult), [xk(c, tb)], [sqk])
                else:
                    act(lambda e, sq=sq, c=c: e.activation(out=sq, in_=xT[:, c, sl], func=AF.Square), [xk(c, tb)], [sqk])
                pend.append((c, sq, sqk))
                flush(2)
            flush(0)
            rs, rsk = tf(8)
            rstd_from(bk, bkk, rs, rsk)
            release(bkk)
            for c in range(KC):
                dve(lambda e, c=c: e.scalar_tensor_tensor(out=hT[:, c, sl], in0=xT[:, c, sl], scalar=pcol(gcol + c), in1=rs,
                                                         op0=ALU.mult, op1=ALU.mult), [xk(c, tb), rsk, "PT"], [hk(c, tb)])

    def resid(tb, gcol, banks):
        sl = slice(tb * 512, (tb + 1) * 512)
        return sl

    def ys_default(idx, dc):
        return YS[:, dc, :], [("ys", dc)]

    def proj_resid(tbs, gcol, nk, rhs_fn, slab_fn, ysf=ys_default):
        nt = len(tbs)
        sbs = [bank(hold=True) for _ in range(nt)]
        pend = []

        def flush(n_keep):
            while len(pend) > n_keep:
                i_, dc_, sq_, sqk_ = pend.pop(0)
                mm(sbs[i_][0][:], onesD[:], sq_, dc_ == 0, dc_ == KC - 1, [sqk_, "onesD"], [sbs[i_][1]], last=True)
        sqn = [0]
        for dc in range(KC):
            lf, wkey = slab_fn(dc)
            for i_ in range(nt):
                bk, bkk = bank()
                for k in range(nk):
                    ra, rk = rhs_fn(k, i_)
                    mm(bk[:], lf(k), ra, k == 0, k == nk - 1, [wkey, rk], [bkk])
                flush(nt)
                ya, yk = ysf(i_, dc)
                act(lambda e, bk=bk, ya=ya: e.activation(out=ya, in_=bk[:], func=AF.Copy), [bkk], yk)
                sq, sqk = tb_(sqn[0] % 4)
                sqn[0] += 1
                act(lambda e, bk=bk, sq=sq: e.activation(out=sq, in_=bk[:], func=AF.Square), [bkk], [sqk])
                pend.append((i_, dc, sq, sqk))
        flush(0)
        for i_, tb in enumerate(tbs):
            sl = slice(tb * 512, (tb + 1) * 512)
            sbk, sbkk = sbs[i_]
            rs, rsk = tf(8 + (i_ % 2))
            rstd_from(sbk, sbkk, rs, rsk)
            release(sbkk)
            for dc in range(KC):
                ya, yk = ysf(i_, dc)
                if dc % 2 == 0:
                    dve(lambda e, dc=dc, ya=ya, rs=rs: e.scalar_tensor_tensor(out=ya, in0=ya, scalar=pcol(gcol + dc), in1=rs,
                                                                            op0=ALU.mult, op1=ALU.mult), yk + [rsk, "PT"], yk)
                    pool(lambda e, dc=dc, ya=ya, sl=sl: e.tensor_tensor(out=xT[:, dc, sl], in0=xT[:, dc, sl], in1=ya, op=ALU.add),
                         yk + [xk(dc, tb)], [xk(dc, tb)])
                else:
                    dve(lambda e, dc=dc, ya=ya, rs=rs: e.tensor_tensor(out=ya, in0=ya, in1=rs, op=ALU.mult), yk + [rsk], yk)
                    dve(lambda e, dc=dc, ya=ya, sl=sl: e.scalar_tensor_tensor(out=xT[:, dc, sl], in0=ya, scalar=pcol(gcol + dc), in1=xT[:, dc, sl],
                                                                            op0=ALU.mult, op1=ALU.add), yk + [xk(dc, tb), "PT"], [xk(dc, tb)])

    def outproj(w_d, gcol):
        wv = w_d[0]
        plan([[(wv[:, s_ * 256:(s_ + 1) * 256], 8, 0, 256, 256)] for _p in range(2) for s_ in range(4)])
        for pp in range(2):
            slabs = {}
            tbs = [2 * pp, 2 * pp + 1]

            def slab_fn(dc):
                s_ = dc // 2
                if s_ not in slabs:
                    slabs[s_] = next_slab()
                buf, key = slabs[s_]
                v = buf[:, 0:2048].rearrange("p (k w) -> p k w", k=8)
                o = (dc % 2) * 128
                return (lambda k: v[:, k, o:o + 128]), key

            def rhs_fn(k, idx, tbs=tbs):
                return mixT(k, tbs[idx])

            def ysf(idx, dc):
                if idx == 0:
                    return YS[:, dc, :], [("ys", dc)]
                return hT[:, dc, 0:1024].bitcast(F32), [hk(dc, 0), hk(dc, 1)]
            proj_resid(tbs, gcol, KC, rhs_fn, slab_fn, ysf)

    Gf = G[:, 16384:22528].bitcast(F32)

    def gf(i):
        return Gf[:, i * 512:(i + 1) * 512], [("G", 32 + 2 * i), ("G", 33 + 2 * i)]

    def next_slab_np():
        if not issued:
            _issue()
        return issued.pop(0)

    def hgrn_setup(hd):
        bufA, wkeyA = next_slab_np()
        bufB, wkeyB = next_slab_np()
        ctx = dict(hd=hd, WA=bufA[:, 0:2048].rearrange("p (k w) -> p k w", k=8), kA=wkeyA,
                   WB=bufB[:, 0:2048].rearrange("p (k w) -> p k w", k=8), kB=wkeyB)
        pool(lambda e: e.memset(Sst[:], 0.0), [], ["Sst"])
        return ctx

    def fset(p):
        ids = (4, 5, 6, 7) if p == 0 else (1, 9, 10, 11)
        return [tb_(i) for i in ids]

    def hgrn_inproj(ctx, tb, coff):
        W, wkey = (ctx["WA"], ctx["kA"]) if coff < 256 else (ctx["WB"], ctx["kB"])
        co = coff % 256
        sl = slice(tb * 512, (tb + 1) * 512)
        bk, bkk = bank(hold=True)
        for k in range(KC):
            mm(bk[:], W[:, k, co:co + 128], hT[:, k, sl], k == 0, k == KC - 1, [wkey, hk(k, tb)], [bkk])
        return bk, bkk

    def hgrn_front(ctx, tb, p):
        hd = ctx["hd"]
        WB, wkeyB = ctx["WB"], ctx["kB"]
        lbc = small[:, hd:hd + 1]
        omlc = small[:, 4 + hd:5 + hd]
        (qd, kqd), (vt, kvt), (kt_, kkt), (sc, ksc) = fset(p)
        bk, bkk = hgrn_inproj(ctx, tb, 128)
        yield
        t_f, kf = tf(0)
        t_l, kl = tf(1)
        t_b, kb = tf(2)
        t_eb, keb = tf(3)
        t_x, kx = tf(4)
        act(lambda e: e.activation(out=t_f, in_=bk[:], func=AF.Exp, scale=-1.0), [bkk], [kf])
        release(bkk)
        bq, bqk = hgrn_inproj(ctx, tb, 0)
        yield
        act(lambda e: e.activation(out=t_f, in_=t_f, func=AF.Ln, bias=1.0, scale=1.0), [kf], [kf])
        yield
        act(lambda e: e.activation(out=t_f, in_=t_f, func=AF.Exp, scale=-1.0), [kf], [kf])
        yield
        dve(lambda e: e.tensor_scalar(out=t_f, in0=t_f, scalar1=omlc, scalar2=lbc, op0=ALU.mult, op1=ALU.add), [kf] + SM, [kf])
        yield
        act(lambda e: e.activation(out=t_l, in_=t_f, func=AF.Ln), [kf], [kl])
        t_q, kq = tf(5)
        act(lambda e: e.activation(out=t_q, in_=bq[:], func=AF.Exp, scale=-1.0), [bqk], [kq])
        yield
        dve(lambda e: e.tensor_scalar(out=t_f, in0=t_f, scalar1=-1.0, scalar2=1.0, op0=ALU.mult, op1=ALU.add), [kf, kl], [kf])
        dve(lambda e: e.tensor_tensor_scan(out=t_b, data0=cmask[:], data1=t_l, initial=0.0, op0=ALU.mult, op1=ALU.add), [kl, "cmask"], [kb])
        act(lambda e: e.activation(out=t_q, in_=t_q, func=AF.Ln, bias=1.0, scale=1.0), [kq], [kq])
        yield
        act(lambda e: e.activation(out=t_eb, in_=t_b, func=AF.Exp), [kb], [keb])
        act(lambda e: e.activation(out=t_l, in_=t_b, func=AF.Exp, scale=-1.0), [kb], [kl])
        yield
        for n in range(8):
            dve(lambda e, n=n: e.tensor_scalar(out=t_x[:, n * 64:(n + 1) * 64], in0=t_l[:, n * 64:(n + 1) * 64],
                                               scalar1=t_eb[:, n * 64 + 63:n * 64 + 64], scalar2=None, op0=ALU.mult), [kl, keb], [kx])
            if n % 4 == 3:
                yield
        act(lambda e: e.activation(out=t_q, in_=t_q, func=AF.Exp, scale=-1.0), [kq], [kq])
        pool(lambda e: e.tensor_copy(out=decs[:, p, :], in_=t_eb.rearrange("p (a b) -> p a b", b=64)[:, :, 63]), [keb], [("decs", p)])
        kd, kkd = tb_(2)
        ke, kke = tb_(3)
        dve(lambda e: e.tensor_tensor(out=kd, in0=t_f, in1=t_l, op=ALU.mult), [kf, kl], [kkd])
        dve(lambda e: e.tensor_tensor(out=ke, in0=t_f, in1=t_x, op=ALU.mult), [kf, kx], [kke])
        yield
        dve(lambda e: e.tensor_tensor(out=t_q, in0=bq[:], in1=t_q, op=ALU.mult), [bqk, kq], [kq])
        release(bqk)
        yield
        dve(lambda e: e.tensor_tensor(out=qd, in0=t_q, in1=t_eb, op=ALU.mult), [kq, keb], [kqd])
        yield
        bk, bkk = bank(hold=True)
        for j in range(4):
            for k in range(KC):
                mm(bk[:, j * 128:(j + 1) * 128], hT[:, k, tb * 512 + j * 128: tb * 512 + (j + 1) * 128], WB[:, k, 0:128],
                   k == 0, k == KC - 1, [wkeyB, hk(k, tb)], [bkk], last=(k == KC - 1 and j == 3))
        yield
        dve(lambda e: e.tensor_copy(out=vt, in_=bk[:]), [bkk], [kvt])
        release(bkk)
        yield
        for j in range(4):
            tr(pbb[:, j * 128:(j + 1) * 128], ke[:, j * 128:(j + 1) * 128], identb[:], [kke, "identb"], ["pbb"], last=(j == 3))
        dve(lambda e: e.tensor_copy(out=kt_, in_=pbb[:, 0:512]), ["pbb"], [kkt])
        yield
        bk, bkk = bank(hold=True)
        for j in range(4):
            mm(bk[:, j * 128:(j + 1) * 128], kd[:, j * 128:(j + 1) * 128], qd[:, j * 128:(j + 1) * 128], True, True,
               [kkd, kqd], [bkk], last=(j == 3))
        dve(lambda e: e.tensor_tensor(out=sc, in0=bk[:], in1=maskh[:], op=ALU.mult), [bkk, "maskh"], [ksc])
        release(bkk)
        yield

    def hgrn_back(ctx, tb, p):
        hd = ctx["hd"]
        (qd, kqd), (vt, kvt), (kt_, kkt), (sc, ksc) = fset(p)
        dk = ("decs", p)
        S2 = TF[:, 1, 0:128]
        S2k = ("tf", 1)
        stt = [(Sst[:], "Sst"), (S2, S2k)]
        bu0, bu0k = bank(hold=True)
        bu1, bu1k = bank(hold=True)
        for n in (0, 2, 4, 6, 1, 3, 5, 7):
            j, hf = n // 2, n % 2
            bu, buk = (bu0, bu0k) if hf == 0 else (bu1, bu1k)
            rows = slice(hf * 64, (hf + 1) * 64)
            mm(bu[:, j * 128:(j + 1) * 128], kt_[rows, j * 128:(j + 1) * 128], vt[rows, j * 128:(j + 1) * 128], True, True,
               [kkt, kvt], [buk], last=(j == 3))
        yield
        pool(lambda e: e.tensor_copy(out=Sb[:, 0, :], in_=Sst[:]), ["Sst"], ["Sb"])
        for n in range(8):
            bu, buk = (bu0, bu0k) if n % 2 == 0 else (bu1, bu1k)
            (si, sik), (so, sok) = stt[n % 2], stt[(n + 1) % 2]
            dve(lambda e, n=n, bu=bu, si=si, so=so: e.scalar_tensor_tensor(out=so, in0=si, scalar=decs[:, p, n:n + 1],
                                                                         in1=bu[:, (n // 2) * 128:(n // 2 + 1) * 128], op0=ALU.mult, op1=ALU.add),
                [sik, dk, buk], [sok])
            if n < 7:
                pool(lambda e, n=n, so=so: e.tensor_copy(out=Sb[:, n + 1, :], in_=so), [sok], ["Sb"])
            yield
        release(bu0k)
        release(bu1k)
        bg, bgk = hgrn_inproj(ctx, tb, 384)
        yield
        bo, bok = bank(hold=True)
        for j in range(4):
            mm(bo[:, j * 128:(j + 1) * 128], vt[:, j * 128:(j + 1) * 128], sc[:, j * 128:(j + 1) * 128], True, False, [kvt, ksc], [bok], last=False)
            mm(bo[:, j * 128:j * 128 + 64], Sb[:, 2 * j, :], qd[:, j * 128:j * 128 + 64], False, False, ["Sb", kqd], [bok], last=False)
            mm(bo[:, j * 128 + 64:(j + 1) * 128], Sb[:, 2 * j + 1, :], qd[:, j * 128 + 64:(j + 1) * 128], False, True, ["Sb", kqd], [bok], last=(j == 3))
        yield
        osq, kosq = tb_(8)
        act(lambda e: e.activation(out=osq, in_=bo[:], func=AF.Square), [bok], [kosq])
        yield
        t_g, kg = tf(7)
        act(lambda e: e.activation(out=t_g, in_=bg[:], func=AF.Exp, scale=-1.0), [bgk], [kg])
        release(bgk)
        bs, bsk = bank(hold=True)
        mm(bs[:], onesH[:], osq, True, True, [kosq, "onesH"], [bsk])
        yield
        act(lambda e: e.activation(out=t_g, in_=t_g, func=AF.Ln, bias=1.0, scale=1.0), [kg], [kg])
        t_r, kr = tf(6)
        act(lambda e: e.activation(out=t_r, in_=bs[:], func=AF.Ln, bias=EPS, scale=1.0), [bsk], [kr])
        release(bsk)
        yield
        act(lambda e: e.activation(out=t_r, in_=t_r, func=AF.Exp, scale=-0.5), [kr], [kr])
        act(lambda e: e.activation(out=t_g, in_=t_g, func=AF.Exp, scale=-1.0), [kg], [kg])
        yield
        t_o, ko = tf(9)
        dve(lambda e: e.scalar_tensor_tensor(out=t_o, in0=bo[:], scalar=pcol(C_HN + hd), in1=t_r, op0=ALU.mult, op1=ALU.mult),
            [bok, kr, "PT"], [ko])
        release(bok)
        yield
        mo, mok = mixT(hd, tb)
        dve(lambda e: e.tensor_tensor(out=mo, in0=t_o, in1=t_g, op=ALU.mult), [ko, kg], [mok])
        yield

    def rglru_setup(rc):
        buf, wkey = next_slab_np()
        pool(lambda e: e.memset(rhalo[:, rc, :], 0.0), [], ["rhalo"])
        pool(lambda e: e.memset(hlast[:, rc:rc + 1], 0.0), [], ["hlast"])
        return dict(rc=rc, W=buf[:, 0:2048].rearrange("p (k w) -> p k w", k=8), k=wkey)

    def rglru_tb(ctx, tb):
        rc, W, wkey = ctx["rc"], ctx["W"], ctx["k"]
        kxb = ("tf", 2)
        sl = slice(tb * 512, (tb + 1) * 512)
        bk, bkk = bank(hold=True)
        for k in range(KC):
            mm(bk[:], W[:, k, 0:128], hT[:, k, sl], k == 0, k == KC - 1, [wkey, hk(k, tb)], [bkk])
        yield
        pool(lambda e: e.tensor_copy(out=TF[:, 2, 0:3], in_=rhalo[:, rc, :]), ["rhalo"], [kxb])
        act(lambda e: e.activation(out=TF[:, 2, 3:515], in_=bk[:], func=AF.Copy), [bkk], [kxb])
        release(bkk)
        pool(lambda e: e.tensor_copy(out=rhalo[:, rc, :], in_=TF[:, 2, 512:515]), [kxb], ["rhalo"])
        yield
        xf, kxf = gf(0)
        act(lambda e: e.activation(out=xf, in_=TF[:, 2, 3:515], func=AF.Identity, scale=pcol(C_RCW + 3 * 4 + rc), bias=pcol(C_RCB + rc)),
            [kxb, "PT"], kxf)
        yield
        for kk in range(3):
            dve(lambda e, kk=kk: e.scalar_tensor_tensor(out=xf, in0=TF[:, 2, kk:kk + 512], scalar=pcol(C_RCW + kk * 4 + rc), in1=xf,
                                                       op0=ALU.mult, op1=ALU.add), [kxb, "PT"] + kxf, kxf)
            yield
        xfb, kxfb = tb_(0)
        pool(lambda e: e.tensor_copy(out=xfb, in_=xf), kxf, [kxfb])
        yield
        br, brk = bank(hold=True)
        mm(br[:], wabd[:, rc, :], xfb, True, True, [kxfb, "wabd"], [brk])
        bi, bik = bank(hold=True)
        mm(bi[:], wxbd[:, rc, :], xfb, True, True, [kxfb, "wxbd"], [bik])
        yield
        t_r, kr = gf(1)
        t_i, ki = gf(2)
        act(lambda e: e.activation(out=t_r, in_=br[:], func=AF.Exp, scale=-1.0, bias=small[:, 44 + rc:45 + rc]), [brk] + SM, kr)
        release(brk)
        act(lambda e: e.activation(out=t_i, in_=bi[:], func=AF.Exp, scale=-1.0, bias=small[:, 48 + rc:49 + rc]), [bik] + SM, ki)
        release(bik)
        yield
        by, byk = bank(hold=True)
        for k in range(KC):
            mm(by[:], W[:, k, 128:256], hT[:, k, sl], k == 0, k == KC - 1, [wkey, hk(k, tb)], [byk])
        act(lambda e: e.activation(out=t_r, in_=t_r, func=AF.Ln, bias=1.0, scale=1.0), kr, kr)
        act(lambda e: e.activation(out=t_i, in_=t_i, func=AF.Ln, bias=1.0, scale=1.0), ki, ki)
        yield
        act(lambda e: e.activation(out=t_r, in_=t_r, func=AF.Exp, scale=-1.0), kr, kr)
        act(lambda e: e.activation(out=t_i, in_=t_i, func=AF.Exp, scale=-1.0), ki, ki)
        t_g, kg = TF[:, 3, 0:512], ("tf", 3)
        act(lambda e: e.activation(out=t_g, in_=by[:], func=AF.Square), [byk], [kg])
        yield
        t_a, ka = gf(3)
        t_a2, ka2 = gf(4)
        act(lambda e: e.activation(out=t_a, in_=t_r, func=AF.Exp, scale=small[:, 24 + rc:25 + rc]), kr + SM, ka)
        act(lambda e: e.activation(out=t_a2, in_=t_r, func=AF.Exp, scale=small[:, 28 + rc:29 + rc]), kr + SM, ka2)
        dve(lambda e: e.tensor_tensor(out=t_i, in0=t_i, in1=xf, op=ALU.mult), ki + kxf, ki)
        dve(lambda e: e.tensor_scalar(out=t_g, in0=t_g, scalar1=0.044715, scalar2=1.0, op0=ALU.mult, op1=ALU.add), [kg], [kg])
        yield
        dve(lambda e: e.tensor_scalar(out=t_a2, in0=t_a2, scalar1=-1.0, scalar2=1.0, op0=ALU.mult, op1=ALU.add), ka2, ka2)
        dve(lambda e: e.tensor_scalar_max(out=t_a2, in0=t_a2, scalar1=1e-30), ka2, ka2)
        dve(lambda e: e.tensor_tensor(out=t_g, in0=by[:], in1=t_g, op=ALU.mult), [byk, kg], [kg])
        yield
        act(lambda e: e.activation(out=t_a2, in_=t_a2, func=AF.Ln), ka2, ka2)
        act(lambda e: e.activation(out=t_g, in_=t_g, func=AF.Exp, scale=-1.5957691216), [kg], [kg])
        yield
        act(lambda e: e.activation(out=t_a2, in_=t_a2, func=AF.Exp, scale=0.5), ka2, ka2)
        act(lambda e: e.activation(out=t_g, in_=t_g, func=AF.Ln, bias=1.0, scale=1.0), [kg], [kg])
        yield
        dve(lambda e: e.tensor_tensor(out=t_i, in0=t_i, in1=t_a2, op=ALU.mult), ki + ka2, ki)
        act(lambda e: e.activation(out=t_g, in_=t_g, func=AF.Exp, scale=-1.0), [kg], [kg])
        yield
        t_h, kh = gf(5)
        dve(lambda e: e.tensor_tensor_scan(out=t_h, data0=t_a, data1=t_i, initial=hlast[:, rc:rc + 1], op0=ALU.mult, op1=ALU.add),
            ka + ki + ["hlast"], kh)
        dve(lambda e: e.tensor_tensor(out=t_g, in0=by[:], in1=t_g, op=ALU.mult), [byk, kg], [kg])
        release(byk)
        yield
        pool(lambda e: e.tensor_copy(out=hlast[:, rc:rc + 1], in_=t_h[:, 511:512]), kh, ["hlast"])
        mo, mok = mixT(4 + rc, tb)
        dve(lambda e: e.tensor_tensor(out=mo, in0=t_h, in1=t_g, op=ALU.mult), kh + [kg], [mok])
        yield

    def interleave(gens):
        gens = list(gens)
        while gens:
            for g in list(gens):
                try:
                    next(g)
                except StopIteration:
                    gens.remove(g)

    def ffn(l):
        wu = fwu_d[l]
        wd = fwd_d[l]
        gcol = C_G + l * 32 + 3 * 8
        pool(lambda e: e.memset(fhalo[:], 0.0), [("fh", q_) for q_ in range(44)], [("fh", q_) for q_ in range(44)])
        ffn_rr = [0, 0]
        for hb in range(2):
            plan([[(wu[:, j * 128:(j + 1) * 128], 8, 0, 256, 128),
                   (wu[:, DFF + j * 128:DFF + (j + 1) * 128], 8, 128, 256, 128)] for j in range(NJ)])
            plan([[(wd[:, dc * 128:(dc + 1) * 128], NJ, 0, 128, 128)] for dc in range(KC)])
        prefetch()
        prenorm(C_G + l * 32 + 16, (0, 1))
        for hb in range(2):
            steps = [(j, sub) for j in range(NJ) for sub in range(2)]
            st = {}
            Wcur = [None, None]

            def S0(i):
                j, sub = steps[i]
                if sub == 0:
                    buf, wkey = next_slab()
                    Wcur[0] = buf[:, 0:2048].rearrange("p (k w) -> p k w", k=8)
                    Wcur[1] = wkey
                W, wkey = Wcur
                tb = hb * 2 + sub
                sl = slice(tb * 512, (tb + 1) * 512)
                rec = []
                for gv in range(2):
                    jj = gv * NJ + j
                    bk, bkk = bank()
                    for k in range(KC):
                        mm(bk[:], W[:, k, gv * 128:(gv + 1) * 128], hT[:, k, sl], k == 0, k == KC - 1, [wkey, hk(k, tb)], [bkk])
                    ts_ = ffn_rr[0] % 4
                    ffn_rr[0] += 1
                    tc_, ktc = tf(ffn_rr[1] % 8)
                    ffn_rr[1] += 1
                    rec.append((jj, bk, bkk, ts_, ("tf", ts_), ("fh", jj), tc_, ktc, C_FCW + l * 132 + jj))
                for (jj, bk, bkk, ts_, kty, fhk, tc_, ktc, cw) in rec:
                    pool(lambda e, ts_=ts_, jj=jj: e.tensor_copy(out=TF[:, ts_, 0:2], in_=fhalo[:, jj, :]), [fhk], [kty])
                for (jj, bk, bkk, ts_, kty, fhk, tc_, ktc, cw) in rec:
                    act(lambda e, bk=bk, ts_=ts_: e.activation(out=TF[:, ts_, 2:514], in_=bk[:], func=AF.Copy), [bkk], [kty])
                for (jj, bk, bkk, ts_, kty, fhk, tc_, ktc, cw) in rec:
                    pool(lambda e, ts_=ts_, jj=jj: e.tensor_copy(out=fhalo[:, jj, :], in_=TF[:, ts_, 512:514]), [kty], [fhk])
                for (jj, bk, bkk, ts_, kty, fhk, tc_, ktc, cw) in rec:
                    act(lambda e, ts_=ts_, tc_=tc_, cw=cw, jj=jj: e.activation(out=tc_, in_=TF[:, ts_, 2:514], func=AF.Identity,
                                                                            scale=pcol(cw + 88), bias=pcol(C_FCB + l * 44 + jj)),
                        [kty, "PT"], [ktc])
                st[i] = rec

            def S1(i):
                rec = st[i]
                for off_, cofs in ((1, 44), (0, 0)):
                    for (jj, bk, bkk, ts_, kty, fhk, tc_, ktc, cw) in rec:
                        dve(lambda e, ts_=ts_, tc_=tc_, cw=cw, off_=off_, cofs=cofs: e.scalar_tensor_tensor(
                            out=tc_, in0=TF[:, ts_, off_:off_ + 512], scalar=pcol(cw + cofs), in1=tc_, op0=ALU.mult, op1=ALU.add),
                            [kty, ktc, "PT"], [ktc])

            def S2(i):
                j, sub = steps[i]
                rec = st.pop(i)
                ga, gak = rec[0][6], rec[0][7]
                vb, vbk = rec[1][6], rec[1][7]
                act(lambda e: e.activation(out=ga, in_=ga, func=AF.Gelu_apprx_tanh), [gak], [gak])
                go, gok = gvT(j, sub)
                dve(lambda e: e.tensor_tensor(out=go, in0=ga, in1=vb, op=ALU.mult), [gak, vbk], [gok])
            n_ = len(steps)
            for t_ in range(n_ + 2):
                if hb == 0 and t_ == 12:
                    prenorm(C_G + l * 32 + 16, (2, 3))
                if t_ < n_:
                    S0(t_)
                if 0 <= t_ - 1 < n_:
                    S1(t_ - 1)
                if 0 <= t_ - 2 < n_:
                    S2(t_ - 2)
            def slab_fn(dc):
                buf, key = next_slab()
                v = buf[:, 0:NJ * 128].rearrange("p (k w) -> p k w", k=NJ)
                return (lambda k: v[:, k, :]), key

            def rhs_fn(k, idx):
                return gvT(k, idx)

            def ysf(idx, dc, hb=hb):
                if idx == 0:
                    return YS[:, dc, :], [("ys", dc)]
                return hT[:, dc, hb * 1024:(hb + 1) * 1024].bitcast(F32), [hk(dc, 2 * hb), hk(dc, 2 * hb + 1)]
            proj_resid([hb * 2, hb * 2 + 1], gcol, NJ, rhs_fn, slab_fn, ysf)

    def fox_plan():
        wv = owi_d[0]
        for pr_ in range(8):
            plan([[(wv[:, 2048 + pr_ * 128:2048 + (pr_ + 1) * 128], 8, 0, 128, 128)],
                  [(wv[:, pr_ * 128:(pr_ + 1) * 128], 8, 0, 256, 128),
                   (wv[:, 1024 + pr_ * 128:1024 + (pr_ + 1) * 128], 8, 128, 256, 128)]])

    def fox():
        wv = owi_d[0]
        pool(lambda e: e.memset(clast[:], 0.0), [], ["clast"])
        for tb in range(NTB):
            sl = slice(tb * 512, (tb + 1) * 512)
            bk, bkk = bank()
            for k in range(KC):
                mm(bk[0:16, :], wfb[:, k, :], hT[:, k, sl], k == 0, k == KC - 1, ["wfb", hk(k, tb)], [bkk])
            ls = TF[0:16, 0, 0:512]
            act(lambda e, bk=bk: e.activation(out=ls, in_=bk[0:16, :], func=AF.Sigmoid, bias=small[0:16, 40:41]), [bkk, "small_ffb"], [("tf", 0)])
            act(lambda e: e.activation(out=ls, in_=ls, func=AF.Ln), [("tf", 0)], [("tf", 0)])
            dve(lambda e, sl=sl: e.tensor_tensor_scan(out=cT[:, sl], data0=onesF[0:16, :], data1=ls, initial=clast[:, 0:1], op0=ALU.mult, op1=ALU.add),
                [("tf", 0), "onesF", "clast"], [("ys", tb)])
            pool(lambda e, tb=tb: e.tensor_copy(out=clast[:, 0:1], in_=cT[:, tb * 512 + 511:tb * 512 + 512]), [("ys", tb)], ["clast"])
            pool(lambda e, sl=sl: e.tensor_copy(out=crefT[:, sl], in_=cT[:, sl]), [("ys", tb)], [("G", 40 + tb)])
        bk, bkk = bank()
        for kt in range(16):
            tr(bk[:, kt * 16:(kt + 1) * 16], cT[:, kt * 128:(kt + 1) * 128], ident[0:16, 0:16], [("ys", kt // 4), "ident"], [bkk], last=(kt == 15))
        act(lambda e, bk=bk: e.activation(out=negc[:].rearrange("p a b -> p (a b)"), in_=bk[:, 0:256], func=AF.Copy, scale=-1.0), [bkk], ["negc"])
        pool(lambda e: e.memset(vaug[:, :, 0, 64:128], 1.0), [bkk], VK)
        pool(lambda e: e.memset(vaug[:, :, 1, 0:64], 1.0), [], VK)
        vaugB = YSb[:, 0:4096].rearrange("p (t e c) -> p t e c", t=16, e=2)
        VKB = [("ys", i) for i in range(0, 4)]
        pool(lambda e: e.memset(vaugB[:, :, 0, 64:128], 1.0), [], VKB)
        pool(lambda e: e.memset(vaugB[:, :, 1, 0:64], 1.0), [], VKB)
        VA, VKS = [vaug, vaugB], [VK, VKB]

        def vproj(pr_):
            va, vk = VA[pr_ % 2], VKS[pr_ % 2]
            bufV, wkeyV = next_slab()
            WV = bufV[:, 0:1024].rearrange("p (k w) -> p k w", k=8)
            for g4 in range(4):
                bk, bkk = bank(hold=True)
                for j in range(4):
                    tt = g4 * 4 + j
                    for k in range(KC):
                        mm(bk[:, j * 128:(j + 1) * 128], hT[:, k, tt * 128:(tt + 1) * 128], WV[:, k, 0:128], k == 0, k == KC - 1,
                           [wkeyV, hk(k, g4)], [bkk], last=(k == KC - 1 and j == 3))
                bv = bk[:].rearrange("p (j e c) -> p j e c", j=4, e=2)
                dve(lambda e, bv=bv, g4=g4, va=va: e.tensor_copy(out=va[:, g4 * 4:(g4 + 1) * 4, 0, 0:64], in_=bv[:, :, 0, :]), [bkk], vk)
                dve(lambda e, bv=bv, g4=g4, va=va: e.tensor_copy(out=va[:, g4 * 4:(g4 + 1) * 4, 1, 64:128], in_=bv[:, :, 1, :]), [bkk], vk)
                release(bkk)
        vproj(0)
        pool(lambda e: e.memset(kaug[64:65, :], 1.0), [], KK)
        qaug2 = TB_[:, 4:8, :].rearrange("p a b -> p (a b)")
        kaug2 = TB_[:, 8:12, :].rearrange("p a b -> p (a b)")
        QK2 = [("tbf", 4 + t) for t in range(4)]
        KK2 = [("tbf", 8 + t) for t in range(4)]
        pool(lambda e: e.memset(kaug2[64:65, :], 1.0), [], KK2)
        QA, KA, QKS, KKS = [qaug, qaug2], [kaug, kaug2], [QK, QK2], [KK, KK2]
        for pr in range(8):
            buf, wkey = next_slab()
            W = buf[:, 0:2048].rearrange("p (k w) -> p k w", k=8)
            for tb in range(NTB):
                sl = slice(tb * 512, (tb + 1) * 512)
                bq, bqk = bank()
                for k in range(KC):
                    mm(bq[:], W[:, k, 0:128], hT[:, k, sl], k == 0, k == KC - 1, [wkey, hk(k, tb)], [bqk])
                act(lambda e, bq=bq, sl=sl: e.activation(out=QA[0][0:64, sl], in_=bq[0:64, :], func=AF.Copy, scale=0.125), [bqk], [QKS[0][tb]])
                act(lambda e, bq=bq, sl=sl: e.activation(out=QA[1][0:64, sl], in_=bq[64:128, :], func=AF.Copy, scale=0.125), [bqk], [QKS[1][tb]])
                bkx, bkxk = bank()
                for k in range(KC):
                    mm(bkx[:], W[:, k, 128:256], hT[:, k, sl], k == 0, k == KC - 1, [wkey, hk(k, tb)], [bkxk])
                dve(lambda e, bkx=bkx, sl=sl: e.tensor_copy(out=KA[0][0:64, sl], in_=bkx[0:64, :]), [bkxk], [KKS[0][tb]])
                dve(lambda e, bkx=bkx, sl=sl: e.tensor_copy(out=KA[1][0:64, sl], in_=bkx[64:128, :]), [bkxk], [KKS[1][tb]])
            for e2 in range(2):
                h_ = pr * 2 + e2
                P.dma("sp", lambda e, j, e2=e2, h_=h_: e.dma_start(out=QA[e2][64:65, :], in_=crefT[h_:h_ + 1, :]),
                      reads=[("G", 40 + t) for t in range(4)], writes=QKS[e2])
            def kt_order(qb):
                n = 4 * qb + 4
                if qb == 0:
                    return list(range(4))
                return [0] + list(range(4 * qb, n)) + list(range(1, 4 * qb))
            its = [(e2, qb, kt) for e2 in range(2) for qb in range(4) for kt in kt_order(qb)]
            LA = 4
            sb_ = {}
            bos = {}
            for i in range(len(its) + LA):
                if i < len(its):
                    e2, qb, kt = its[i]
                    qaug_, kaug_, QK_, KK_ = QA[e2], KA[e2], QKS[e2], KKS[e2]
                    j = kt - 4 * qb
                    off = 128 * j if j > 0 else 0
                    N = 512 - off
                    bs, bsk = bank(hold=True)
                    mm(bs[:, 0:N], kaug_[0:65, kt * 128:(kt + 1) * 128], qaug_[0:65, qb * 512 + off:(qb + 1) * 512], True, True,
                       [KK_[kt // 4], QK_[qb]], [bsk])
                    sb_[i] = (bs, bsk, off, N, j)
                if i == 24 and pr + 1 < 8:
                    vproj(pr + 1)
                i2 = i - LA
                if i2 >= 0:
                    e2, qb, kt = its[i2]
                    h = pr * 2 + e2
                    order = kt_order(qb)
                    first, last_ = (kt == order[0]), (kt == order[-1])
                    bs, bsk, off, N, j = sb_.pop(i2)
                    if first:
                        bos[(e2, qb)] = bank(hold=True)
                    bo, bok = bos[(e2, qb)]
                    pt, ptk = tb_(i2 % 4)
                    act(lambda e, bs=bs, pt=pt, N=N, kt=kt, h=h: e.activation(out=pt[:, 0:N], in_=bs[:, 0:N], func=AF.Exp, bias=negc[:, kt, h:h + 1]),
                        [bsk, "negc"], [ptk])
                    release(bsk)
                    if j >= 0:
                        pool(lambda e, pt=pt: e.tensor_tensor(out=pt[:, 0:128], in0=pt[:, 0:128], in1=tri[:], op=ALU.mult), [ptk, "tri"], [ptk])
                    mm(bo[:, off:512], VA[pr % 2][:, kt, e2, :], pt[:, 0:N], first, last_, VKS[pr % 2] + [ptk], [bok], last=True)
                    if last_:
                        rr, rrk = tf(8 + (qb % 2))
                        cp, cpk = TF[:, 2 + (qb % 2), 0:512], ("tf", 2 + (qb % 2))
                        mo, mok = mixT(pr, qb)
                        dve(lambda e, bo=bo, cp=cp: e.tensor_copy(out=cp, in_=bo[:]), [bok], [cpk])
                        release(bok)
                        if e2 == 0:
                            dve(lambda e, cp=cp, rr=rr: e.reciprocal(out=rr[0:64, :], in_=cp[64:128, :]), [cpk], [rrk])
                            dve(lambda e, cp=cp, mo=mo, rr=rr: e.tensor_tensor(out=mo[0:64, :], in0=cp[0:64, :], in1=rr[0:64, :], op=ALU.mult), [cpk, rrk], [mok])
                        else:
                            dve(lambda e, cp=cp, rr=rr: e.reciprocal(out=rr[64:128, :], in_=cp[0:64, :]), [cpk], [rrk])
                            dve(lambda e, cp=cp, mo=mo, rr=rr: e.tensor_tensor(out=mo[64:128, :], in0=cp[64:128, :], in1=rr[64:128, :], op=ALU.mult), [cpk, rrk], [mok])

    setup()
    setup2()
    for s in range(nseq):
        load_x(s)
        import os as _os
        parts = _os.environ.get("KPARTS", "hgrn,rglru,outproj").split(",")
        if nph >= 1:
            prenorm(C_G + 0, (0,))
            wv0 = ewi_d[0]
            hspecA = lambda hd: [(wv0[:, c0:c0 + 128], 8, j * 128, 256, 128) for j, c0 in enumerate([hd * 128, 512 + hd * 128])]
            hspecB = lambda hd: [(wv0[:, c0:c0 + 128], 8, j * 128, 256, 128) for j, c0 in enumerate([1024 + hd * 128, 1536 + hd * 128])]
            rspec = lambda rc: [(wv0[:, 2048 + rc * 128:2048 + (rc + 1) * 128], 8, 0, 256, 128),
                                (wv0[:, 2560 + rc * 128:2560 + (rc + 1) * 128], 8, 128, 256, 128)]
            for i_ in range(4):
                plan([hspecA(i_), hspecB(i_), rspec(i_)])
                ch = hgrn_setup(i_)
                cr = rglru_setup(i_)
                for st_ in range(5):
                    gens = []
                    if st_ >= 1:
                        gens.append(hgrn_back(ch, st_ - 1, (st_ - 1) % 2))
                    if st_ < 4:
                        gens.append(hgrn_front(ch, st_, st_ % 2))
                        gens.append(rglru_tb(cr, st_))
                    if i_ == 0 and st_ < 3:
                        prenorm(C_G + 0, (st_ + 1,), (0, 2, 3, 8))
                    interleave(gens)
            if "outproj" in parts:
                outproj(ewo_d, C_G + 8)
        if nph >= 2:
            ffn(0)
        if nph >= 3:
            fox_plan()
            prefetch()
            prenorm(C_G + 32)
            fox()
            outproj(owo_d, C_G + 32 + 8)
        if nph >= 4:
            ffn(1)
        store_x(s)
    P.wait_all("sp")
    P.build()
    return nc, P


_CACHE = {}


def kernel(**inputs):
    if "nc" not in _CACHE:
        _CACHE["nc"] = build_nc()[0]
    nc = _CACHE["nc"]
    x = np.ascontiguousarray(inputs["x"], dtype=np.float32)
    in_maps = []
    for c in range(NCORES):
        m = {k: np.ascontiguousarray(v, dtype=np.float32) for k, v in inputs.items() if k != "x"}
        m["x"] = np.ascontiguousarray(x[c * SPC:(c + 1) * SPC])
        in_maps.append(m)
    res = run_bass_kernel_spmd(nc, in_maps, core_ids=list(range(NCORES)))
    out = np.concatenate([np.asarray(r["out"]) for r in res.results], axis=0)
    return out.astype(np.float32)
```
